# Optimizing a Trainium2 kernel written in Bass

```python
import math
import jax, jax.numpy as jnp
from jax import lax
import numpy as np

D_MODEL = 2048
BATCH = 2
SEQ = 4096
DEPTH = 2

N_A_LAYERS = DEPTH // 2
N_B_LAYERS = DEPTH - N_A_LAYERS
S5_WIDTH = D_MODEL
S5_GROUP = 16
S5_GROUPS = S5_WIDTH // S5_GROUP
S5_STATE = 64
DT_MIN = 1e-3
DT_MAX = 1e-1
FOX_HEAD_DIM = 128
FOX_HEADS = D_MODEL // FOX_HEAD_DIM
FOX_WIDTH = FOX_HEADS * FOX_HEAD_DIM
Q_BLOCK = 128
RMS_EPS = 1e-6
NEG_INF = -1e30

kernel_name = "yoco_s5_fox_hybrid"

F32 = jnp.float32


def rmsnorm(x, g):
    xf = x.astype(F32)
    y = xf * lax.rsqrt(jnp.mean(xf * xf, axis=-1, keepdims=True) + RMS_EPS)
    return (y * g.astype(F32)).astype(x.dtype)


def s5_ssm(u, a_re, a_im, log_dt, b_re, b_im, c_re, c_im, d_skip):
    bsz, seq, _ = u.shape
    uf = u.astype(F32).reshape(bsz, seq, S5_GROUPS, S5_GROUP)
    dt = jnp.exp(log_dt.astype(F32))[:, None]
    ar = a_re.astype(F32)
    ai = a_im.astype(F32)
    mag = jnp.exp(ar * dt)
    abar_re = mag * jnp.cos(ai * dt)
    abar_im = mag * jnp.sin(ai * dt)
    den = ar * ar + ai * ai
    nr = abar_re - 1.0
    coef_re = (nr * ar + abar_im * ai) / den
    coef_im = (abar_im * ar - nr * ai) / den
    bu_re = jnp.einsum('bsgc,gpc->bsgp', uf, b_re.astype(F32))
    bu_im = jnp.einsum('bsgc,gpc->bsgp', uf, b_im.astype(F32))
    x_re = coef_re * bu_re - coef_im * bu_im
    x_im = coef_re * bu_im + coef_im * bu_re
    shape_a = (1, seq, S5_GROUPS, S5_STATE)
    a_seq_re = jnp.broadcast_to(abar_re, shape_a)
    a_seq_im = jnp.broadcast_to(abar_im, shape_a)

    def combine(left, right):
        a1r, a1i, b1r, b1i = left
        a2r, a2i, b2r, b2i = right
        return (a2r * a1r - a2i * a1i,
                a2r * a1i + a2i * a1r,
                a2r * b1r - a2i * b1i + b2r,
                a2r * b1i + a2i * b1r + b2i)

    _, _, h_re, h_im = lax.associative_scan(combine, (a_seq_re, a_seq_im, x_re, x_im), axis=1)
    y = (jnp.einsum('bsgp,gcp->bsgc', h_re, c_re.astype(F32))
         - jnp.einsum('bsgp,gcp->bsgc', h_im, c_im.astype(F32)))
    y = y + d_skip.astype(F32).reshape(S5_GROUPS, S5_GROUP) * uf
    return y.reshape(bsz, seq, S5_WIDTH)


def s5_layer(h, g_pre, g_post, w_in, a_re, a_im, log_dt, b_re, b_im, c_re, c_im, d_skip, w_glu, b_glu, w_out):
    xn = rmsnorm(h, g_pre)
    uz = xn @ w_in
    u, z = jnp.split(uz, 2, axis=-1)
    y = s5_ssm(u, a_re, a_im, log_dt, b_re, b_im, c_re, c_im, d_skip)
    y = jax.nn.gelu(y)
    y = y * jax.nn.sigmoid(y @ w_glu.astype(F32) + b_glu.astype(F32))
    y = y.astype(h.dtype) * jax.nn.silu(z)
    return h + rmsnorm(y @ w_out, g_post)


def shared_kv(h, g_kv, w_kv, b_f):
    bsz, seq, _ = h.shape
    kvf = rmsnorm(h, g_kv) @ w_kv
    k = kvf[..., :FOX_WIDTH]
    v = kvf[..., FOX_WIDTH:2 * FOX_WIDTH]
    f_logit = kvf[..., 2 * FOX_WIDTH:]
    k = k.reshape(bsz, seq, FOX_HEADS, FOX_HEAD_DIM).transpose(0, 2, 1, 3)
    v = v.reshape(bsz, seq, FOX_HEADS, FOX_HEAD_DIM).transpose(0, 2, 1, 3)
    log_f = jax.nn.log_sigmoid(f_logit.astype(F32) + b_f.astype(F32))
    cum = jnp.cumsum(log_f, axis=1).transpose(0, 2, 1)
    return k, v, cum


def fox_attention(q, k, v, cum):
    bsz, nh, seq, dh = q.shape
    nblk = seq // Q_BLOCK
    qb = q.reshape(bsz, nh, nblk, Q_BLOCK, dh).transpose(2, 0, 1, 3, 4)
    cb = cum.reshape(bsz, nh, nblk, Q_BLOCK).transpose(2, 0, 1, 3)
    kpos = jnp.arange(seq)
    scale = dh ** -0.5

    def block(args):
        q_i, c_i, i = args
        s = jnp.einsum('bhqd,bhkd->bhqk', q_i, k, preferred_element_type=F32) * scale
        s = s + c_i[..., :, None] - cum[:, :, None, :]
        qpos = i * Q_BLOCK + jnp.arange(Q_BLOCK)
        s = jnp.where(kpos[None, :] <= qpos[:, None], s, NEG_INF)
        p = jax.nn.softmax(s, axis=-1)
        return jnp.einsum('bhqk,bhkd->bhqd', p.astype(v.dtype), v)

    o = lax.map(block, (qb, cb, jnp.arange(nblk)))
    return o.transpose(1, 2, 0, 3, 4).reshape(bsz, nh, seq, dh)


def fox_layer(h, g_pre, g_post, w_in, w_out, k, v, cum):
    bsz, seq, _ = h.shape
    qz = rmsnorm(h, g_pre) @ w_in
    q, z = jnp.split(qz, 2, axis=-1)
    q = q.reshape(bsz, seq, FOX_HEADS, FOX_HEAD_DIM).transpose(0, 2, 1, 3)
    o = fox_attention(q, k, v, cum)
    o = o.transpose(0, 2, 1, 3).reshape(bsz, seq, FOX_WIDTH)
    o = o.astype(h.dtype) * jax.nn.silu(z)
    return h + rmsnorm(o @ w_out, g_post)


def setup_inputs(seed: int = 0) -> dict:
    key = jax.random.key(seed)
    ks = jax.random.split(key, 24)
    nrm = lambda k, shp, s: jax.random.normal(k, shp, F32) * s
    n = jnp.arange(S5_STATE, dtype=F32)
    a_re = -0.5 + nrm(ks[4], (N_A_LAYERS, S5_GROUPS, S5_STATE), 0.01)
    a_im = math.pi * n + nrm(ks[5], (N_A_LAYERS, S5_GROUPS, S5_STATE), 0.01)
    log_dt = jax.random.uniform(ks[6], (N_A_LAYERS, S5_GROUPS), F32, math.log(DT_MIN), math.log(DT_MAX))
    return {
        "x": nrm(ks[0], (BATCH, SEQ, D_MODEL), 1.0),
        "norm_pre": 1.0 + nrm(ks[1], (DEPTH, D_MODEL), 0.02),
        "norm_post": 1.0 + nrm(ks[2], (DEPTH, D_MODEL), 0.02),
        "s5_w_in": nrm(ks[3], (N_A_LAYERS, D_MODEL, 2 * S5_WIDTH), D_MODEL ** -0.5),
        "s5_a_re": a_re,
        "s5_a_im": a_im,
        "s5_log_dt": log_dt,
        "s5_b_re": nrm(ks[7], (N_A_LAYERS, S5_GROUPS, S5_STATE, S5_GROUP), (2 * S5_GROUP) ** -0.5),
        "s5_b_im": nrm(ks[8], (N_A_LAYERS, S5_GROUPS, S5_STATE, S5_GROUP), (2 * S5_GROUP) ** -0.5),
        "s5_c_re": nrm(ks[9], (N_A_LAYERS, S5_GROUPS, S5_GROUP, S5_STATE), S5_STATE ** -0.5),
        "s5_c_im": nrm(ks[10], (N_A_LAYERS, S5_GROUPS, S5_GROUP, S5_STATE), S5_STATE ** -0.5),
        "s5_d": nrm(ks[11], (N_A_LAYERS, S5_WIDTH), 1.0),
        "s5_w_glu": nrm(ks[12], (N_A_LAYERS, S5_WIDTH, S5_WIDTH), S5_WIDTH ** -0.5),
        "s5_b_glu": nrm(ks[13], (N_A_LAYERS, S5_WIDTH), 0.01),
        "s5_w_out": nrm(ks[14], (N_A_LAYERS, S5_WIDTH, D_MODEL), S5_WIDTH ** -0.5),
        "kv_norm": 1.0 + nrm(ks[15], (D_MODEL,), 0.02),
        "kv_w": nrm(ks[16], (D_MODEL, 2 * FOX_WIDTH + FOX_HEADS), D_MODEL ** -0.5),
        "kv_b_f": jax.random.uniform(ks[17], (FOX_HEADS,), F32, 2.0, 7.0),
        "fox_w_in": nrm(ks[18], (N_B_LAYERS, D_MODEL, 2 * FOX_WIDTH), D_MODEL ** -0.5),
        "fox_w_out": nrm(ks[19], (N_B_LAYERS, FOX_WIDTH, D_MODEL), FOX_WIDTH ** -0.5),
    }


def reference(x, norm_pre, norm_post, s5_w_in, s5_a_re, s5_a_im, s5_log_dt, s5_b_re, s5_b_im,
              s5_c_re, s5_c_im, s5_d, s5_w_glu, s5_b_glu, s5_w_out, kv_norm, kv_w, kv_b_f,
              fox_w_in, fox_w_out):
    h = x
    k = v = cum = None
    for layer in range(DEPTH):
        if layer < N_A_LAYERS:
            i = layer
            h = s5_layer(h, norm_pre[layer], norm_post[layer], s5_w_in[i], s5_a_re[i], s5_a_im[i],
                         s5_log_dt[i], s5_b_re[i], s5_b_im[i], s5_c_re[i], s5_c_im[i], s5_d[i],
                         s5_w_glu[i], s5_b_glu[i], s5_w_out[i])
        else:
            if layer == N_A_LAYERS:
                k, v, cum = shared_kv(h, kv_norm, kv_w, kv_b_f)
            j = layer - N_A_LAYERS
            h = fox_layer(h, norm_pre[layer], norm_post[layer], fox_w_in[j], fox_w_out[j], k, v, cum)
    return h
```

```python
import numpy as np
from contextlib import ExitStack
import concourse.bass as bass
import concourse.mybir as mybir
from concourse.bass_utils import run_bass_kernel_spmd

F32 = mybir.dt.float32
BF16 = mybir.dt.bfloat16
AF = mybir.ActivationFunctionType
ALU = mybir.AluOpType

NCORES = 8
D = 2048
TOK = 1024
RMS_EPS = 1e-6


class Ctx:
    def __init__(self, nc, es):
        self.nc = nc
        self.es = es
        self.eng = {"pe": nc.tensor, "act": nc.scalar, "dve": nc.vector,
                    "pool": nc.gpsimd, "sp": nc.sync}
        self.psem = {}
        self.cnt = {}
        for e in ("pe", "act", "dve", "pool"):
            self.psem[e] = es.enter_context(nc.semaphore("p_" + e))
            self.cnt[e] = 0
        self.waited = {e: {} for e in self.eng}
        self.dsem = {}

    def sb(self, name, shape, dt):
        return self.es.enter_context(self.nc.sbuf_tensor(name, shape, dt))

    def ps(self, name, shape, dt=F32):
        return self.es.enter_context(self.nc.psum_tensor(name, shape, dt))

    def wait(self, e, deps):
        for t in deps:
            if t is None:
                continue
            if isinstance(t, list):
                self.wait(e, t)
                continue
            sem, val, key = t
            if self.waited[e].get(key, 0) >= val:
                continue
            self.eng[e].wait_ge(sem, val)
            self.waited[e][key] = val

    def op(self, e, fn, deps=()):
        self.wait(e, deps)
        ins = fn(self.eng[e])
        self.cnt[e] += 1
        ins.then_inc(self.psem[e], 1)
        return (self.psem[e], self.cnt[e], "p_" + e)

    def dma(self, q, out, in_, key, deps=(), **kw):
        self.wait(q, deps)
        if key not in self.dsem:
            self.dsem[key] = [self.es.enter_context(self.nc.semaphore("d_" + key)), 0]
        s = self.dsem[key]
        ins = self.eng[q].dma_start(out=out, in_=in_, **kw)
        s[1] += 16
        ins.then_inc(s[0], 16)
        return (s[0], s[1], "d_" + key)


class LinearT:
    def __init__(self, cx, name, KT, mwmax=128, nstage=3):
        self.cx = cx
        self.KT = KT
        self.nstage = nstage
        self.wst = cx.sb(name + "_wst", [128, nstage, KT, mwmax], F32)
        self.wbf = cx.sb(name + "_wbf", [128, 2, KT, mwmax], BF16)
        self.conv_tok = {}
        self.mm_tok = {}
        self.gen = 0
        self.name = name

    def run(self, act, act_deps, W, N, banks, bank_tok, epi, post=None, ntok=1024, conv_eng="pool"):
        cx = self.cx
        KT = self.KT
        Wv = W.rearrange("(kt p) n -> p kt n", p=128)
        n_tiles = (N + 127) // 128
        nh = ntok // 512
        pending_post = None
        for ft in range(n_tiles):
            g = self.gen
            self.gen += 1
            mw = min(128, N - ft * 128)
            s3 = g % self.nstage
            s2 = g % 2
            t_ld = cx.dma("sp", self.wst[:, s3, :, :mw], Wv[:, :, ft * 128: ft * 128 + mw],
                          key=f"{self.name}w{s3}", deps=[self.conv_tok.get(g - self.nstage)])
            self.conv_tok[g] = cx.op(
                conv_eng, lambda e: e.tensor_copy(out=self.wbf[:, s2, :, :mw], in_=self.wst[:, s3, :, :mw]),
                deps=[t_ld, self.mm_tok.get(g - 2)])
            last = None
            for h in range(nh):
                bi = (2 * g + h) % len(banks)
                bank = banks[bi]
                for kt in range(KT):
                    last = cx.op("pe", lambda e: e.matmul(
                        bank[:mw, :], lhsT=self.wbf[:, s2, kt, :mw], rhs=act[:, kt, h * 512:(h + 1) * 512],
                        start=(kt == 0), stop=(kt == KT - 1)),
                        deps=[self.conv_tok[g], bank_tok[bi], act_deps] if kt == 0 else [])
                bank_tok[bi] = epi(ft, h, bank[:mw, :], mw, last)
            self.mm_tok[g] = last
            if pending_post is not None:
                pending_post()
            pending_post = (lambda f=ft: post(f)) if post is not None else None
        if pending_post is not None:
            pending_post()


def rstd_from_banks(cx, ssq_banks, rstd, last, ntok=1024):
    t = None
    for h in range(ntok // 512):
        t = cx.op("dve", lambda e: e.tensor_scalar(out=rstd[:, h * 512:(h + 1) * 512], in0=ssq_banks[h][:, :],
                                                   scalar1=RMS_EPS, scalar2=None, op0=ALU.add), deps=[last])
    t = cx.op("act", lambda e: e.activation(out=rstd[:, :ntok], in_=rstd[:, :ntok], func=AF.Sqrt), deps=[t])
    t = cx.op("dve", lambda e: e.reciprocal(out=rstd[:, :ntok], in_=rstd[:, :ntok]), deps=[t])
    return t


def rms_stats(cx, src_fn, n_kt, ones, sq, ssq_banks, rstd, src_deps, ntok=1024):
    nslot = sq.shape[1]
    sq_free = [None] * nslot
    last = None
    nh = ntok // 512
    for kt in range(n_kt):
        sl = kt % nslot
        t_sq = cx.op("act", lambda e: e.activation(out=sq[:, sl, :ntok], in_=src_fn(kt), func=AF.Square),
                     deps=[sq_free[sl], src_deps(kt)])
        for h in range(nh):
            last = cx.op("pe", lambda e: e.matmul(ssq_banks[h][:, :], lhsT=ones[:, :],
                                                  rhs=sq[:, sl, h * 512:(h + 1) * 512],
                                                  start=(kt == 0), stop=(kt == n_kt - 1)),
                         deps=[t_sq])
        sq_free[sl] = last
    t = rstd_from_banks(cx, ssq_banks, rstd, last, ntok)
    return t


def build_norm_linear(N):
    nc = bass.Bass("TRN2", target_bir_lowering=False)
    xT = nc.dram_tensor("xT", [D, TOK], F32, kind="ExternalInput").ap()
    gcol = nc.dram_tensor("gcol", [128, 16], F32, kind="ExternalInput").ap()
    w = nc.dram_tensor("w", [D, N], F32, kind="ExternalInput").ap()
    outT = nc.dram_tensor("outT", [N, TOK], F32, kind="ExternalOutput").ap()
    with ExitStack() as es:
        cx = Ctx(nc, es)
        xs = cx.sb("xs", [128, 16, TOK], F32)
        xn = cx.sb("xn", [128, 16, TOK], BF16)
        sq = cx.sb("sq", [128, 4, TOK], BF16)
        rstd = cx.sb("rstd", [128, TOK], F32)
        ones = cx.sb("ones", [128, 128], BF16)
        gc = cx.sb("gc", [128, 16], F32)
        ost = cx.sb("ost", [128, 4, 512], F32)
        ssq = [cx.ps(f"ssq{h}", [128, 512]) for h in range(2)]
        banks = [cx.ps(f"acc{i}", [128, 512]) for i in range(4)]
        lin = LinearT(cx, "l", 16)

        t_ones = cx.op("pool", lambda e: e.memset(ones[:, :], 1.0 / D))
        t_g = cx.dma("sp", gc[:, :], gcol[:, :], key="g")
        xv = xT.rearrange("(kt p) t -> p kt t", p=128)
        t_x = [cx.dma("sp", xs[:, 4 * i:4 * i + 4, :], xv[:, 4 * i:4 * i + 4, :], key=f"x{i}") for i in range(4)]
        t_r = rms_stats(cx, lambda kt: xs[:, kt, :], 16, ones, sq, ssq, rstd,
                        lambda kt: [t_x[kt // 4], t_ones])
        t_xn = None
        for kt in range(16):
            t_xn = cx.op("dve", lambda e: e.scalar_tensor_tensor(
                out=xn[:, kt, :], in0=xs[:, kt, :], scalar=gc[:, kt:kt + 1], in1=rstd[:, :],
                op0=ALU.mult, op1=ALU.mult), deps=[t_r, t_g])

        ost_free = [None] * 4
        st = {"i": 0, "last": None}

        def epi(ft, h, ps, mw, tok):
            sl = st["i"] % 4
            st["i"] += 1
            t_c = cx.op("act", lambda e: e.activation(out=ost[:mw, sl, :], in_=ps, func=AF.Copy),
                        deps=[tok, ost_free[sl]])
            t_d = cx.dma("act", outT[ft * 128: ft * 128 + mw, h * 512:(h + 1) * 512], ost[:mw, sl, :],
                         key=f"o{sl}", deps=[t_c])
            ost_free[sl] = t_d
            st["last"] = t_d
            return t_c

        lin.run(xn, [t_xn], w, N, banks, [None] * 4, epi)
        cx.wait("act", [t for t in ost_free])
    return nc


def _tok_shard_T(a2d):
    return [np.ascontiguousarray(a2d[c * TOK:(c + 1) * TOK].T) for c in range(NCORES)]


def _gcol(g):
    return np.ascontiguousarray(g.reshape(16, 128).T)


def run_norm_linear(x2d, g, w):
    N = w.shape[1]
    nc = build_norm_linear(N)
    xTs = _tok_shard_T(x2d)
    gc = _gcol(g)
    wc = np.ascontiguousarray(w)
    res = run_bass_kernel_spmd(nc, [{"xT": xTs[c], "gcol": gc, "w": wc} for c in range(NCORES)],
                               core_ids=list(range(NCORES)))
    return np.concatenate([r["outT"].T for r in res.results], axis=0)


def build_gate_out(glu):
    nc = bass.Bass("TRN2", target_bir_lowering=False)
    aT = nc.dram_tensor("aT", [D, TOK], F32, kind="ExternalInput").ap()
    zT = nc.dram_tensor("zT", [D, TOK], F32, kind="ExternalInput").ap()
    resT = nc.dram_tensor("resT", [D, TOK], F32, kind="ExternalInput").ap()
    gcol = nc.dram_tensor("gcol", [128, 16], F32, kind="ExternalInput").ap()
    w_out = nc.dram_tensor("w_out", [D, D], F32, kind="ExternalInput").ap()
    if glu:
        w_glu = nc.dram_tensor("w_glu", [D, D], F32, kind="ExternalInput").ap()
        bcol = nc.dram_tensor("bcol", [128, 16], F32, kind="ExternalInput").ap()
    hT = nc.dram_tensor("hT", [D, TOK], F32, kind="ExternalOutput").ap()
    av = aT.rearrange("(kt p) t -> p kt t", p=128)
    zv = zT.rearrange("(kt p) t -> p kt t", p=128)
    rv = resT.rearrange("(kt p) t -> p kt t", p=128)
    hv = hT.rearrange("(kt p) t -> p kt t", p=128)
    with ExitStack() as es:
        cx = Ctx(nc, es)
        abf = cx.sb("abf", [128, 16, TOK], BF16)
        sz = cx.sb("sz", [128, 16, TOK], BF16)
        mT = cx.sb("mT", [128, 16, TOK], F32)
        sq = cx.sb("sq", [128, 4, 512], BF16)
        sig = cx.sb("sig", [128, 4, 512], BF16)
        ld = cx.sb("ld", [128, 4, TOK], F32)
        rstd = cx.sb("rstd", [128, TOK], F32)
        ones = cx.sb("ones", [128, 128], BF16)
        gc = cx.sb("gc", [128, 16], F32)
        bc = cx.sb("bc", [128, 16], F32)
        ssq = [cx.ps(f"ssq{h}", [128, 512]) for h in range(2)]
        banks = [cx.ps(f"acc{i}", [128, 512]) for i in range(4)]
        lin = LinearT(cx, "l", 16, nstage=2)

        t_ones = cx.op("pool", lambda e: e.memset(ones[:, :], 1.0 / D))
        t_g = cx.dma("sp", gc[:, :], gcol[:, :], key="g")
        t_b = cx.dma("sp", bc[:, :], bcol[:, :], key="g") if glu else None
        ld_free = [None] * 4
        t_y2 = []
        for kt in range(16):
            sa, sz_ = (2 * kt) % 4, (2 * kt + 1) % 4
            t1 = cx.dma("sp", ld[:, sa, :], av[:, kt, :], key=f"ld{sa}", deps=[ld_free[sa]])
            t2 = cx.dma("sp", ld[:, sz_, :], zv[:, kt, :], key=f"ld{sz_}", deps=[ld_free[sz_]])
            ta = cx.op("dve", lambda e: e.tensor_copy(out=abf[:, kt, :], in_=ld[:, sa, :]), deps=[t1])
            tz = cx.op("act", lambda e: e.activation(out=sz[:, kt, :], in_=ld[:, sz_, :], func=AF.Silu), deps=[t2])
            ld_free[sa], ld_free[sz_] = ta, tz
            if not glu:
                ty = cx.op("dve", lambda e: e.tensor_tensor(out=sz[:, kt, :], in0=abf[:, kt, :], in1=sz[:, kt, :],
                                                            op=ALU.mult), deps=[ta, tz])
            else:
                ty = [ta, tz]
            t_y2.append(ty)

        bank_tok = [None] * 4
        if glu:
            st = {"i": 0}
            sig_free = [None] * 4
            y2 = []

            def epi_glu(ft, h, ps, mw, tok):
                sl = st["i"] % 4
                st["i"] += 1
                t_s = cx.op("act", lambda e: e.activation(out=sig[:, sl, :], in_=ps, func=AF.Sigmoid,
                                                          bias=bc[:, ft:ft + 1]), deps=[tok, sig_free[sl], t_b])
                hs = slice(h * 512, (h + 1) * 512)
                t_p = cx.op("dve", lambda e: e.tensor_tensor(out=sig[:, sl, :], in0=sig[:, sl, :], in1=abf[:, ft, hs],
                                                             op=ALU.mult), deps=[t_s])
                t_p = cx.op("dve", lambda e: e.tensor_tensor(out=sz[:, ft, hs], in0=sig[:, sl, :], in1=sz[:, ft, hs],
                                                             op=ALU.mult), deps=[t_p])
                sig_free[sl] = t_p
                y2.append(t_p)
                return t_s

            lin.run(abf, t_y2, w_glu, D, banks, bank_tok, epi_glu)
            y2_deps = y2
        else:
            y2_deps = t_y2

        st2 = {"i": 0}
        sq_free = [None] * 4
        sq_tok = {}

        def epi_out(ft, h, ps, mw, tok):
            sl = st2["i"] % 4
            st2["i"] += 1
            hs = slice(h * 512, (h + 1) * 512)
            cx.op("act", lambda e: e.activation(out=mT[:, ft, hs], in_=ps, func=AF.Copy), deps=[tok])
            t_q = cx.op("act", lambda e: e.activation(out=sq[:, sl, :], in_=ps, func=AF.Square), deps=[sq_free[sl]])
            sq_tok[(ft, h)] = (t_q, sl)
            return t_q

        fin = {"t": None}

        def post(ft):
            for h in range(2):
                t_q, sl = sq_tok[(ft, h)]
                t = cx.op("pe", lambda e: e.matmul(ssq[h][:, :], lhsT=ones[:, :], rhs=sq[:, sl, :],
                                                   start=(ft == 0), stop=(ft == 15)), deps=[t_q, t_ones])
                sq_free[sl] = t
                fin["t"] = t

        lin.run(sz, y2_deps, w_out, D, banks, bank_tok, epi_out, post=post)
        t_r = rstd_from_banks(cx, ssq, rstd, fin["t"])
        outs = []
        for ft in range(16):
            sr, so = (2 * ft) % 4, (2 * ft + 1) % 4
            t1 = cx.dma("sp", ld[:, sr, :], rv[:, ft, :], key=f"ld{sr}", deps=[ld_free[sr]])
            tm = cx.op("dve", lambda e: e.scalar_tensor_tensor(out=ld[:, so, :], in0=mT[:, ft, :], scalar=gc[:, ft:ft + 1],
                                                               in1=rstd[:, :], op0=ALU.mult, op1=ALU.mult),
                       deps=[t_r, t_g, ld_free[so]])
            ta = cx.op("dve", lambda e: e.tensor_tensor(out=ld[:, so, :], in0=ld[:, so, :], in1=ld[:, sr, :], op=ALU.add),
                       deps=[tm, t1])
            td = cx.dma("act", hv[:, ft, :], ld[:, so, :], key=f"ho{so}", deps=[ta])
            ld_free[sr] = ta
            ld_free[so] = td
            outs.append(td)
        cx.wait("act", outs)
    return nc


def run_gate_out(a2d, z2d, res2d, g, w_out, w_glu=None, b_glu=None):
    glu = w_glu is not None
    nc = build_gate_out(glu)
    aTs, zTs, rTs = _tok_shard_T(a2d), _tok_shard_T(z2d), _tok_shard_T(res2d)
    base = {"gcol": _gcol(g), "w_out": np.ascontiguousarray(w_out)}
    if glu:
        base["w_glu"] = np.ascontiguousarray(w_glu)
        base["bcol"] = _gcol(b_glu)
    res = run_bass_kernel_spmd(nc, [dict(base, aT=aTs[c], zT=zTs[c], resT=rTs[c]) for c in range(NCORES)],
                               core_ids=list(range(NCORES)))
    return np.concatenate([r["hT"].T for r in res.results], axis=0)


def build_s5(GPC, NCH, NB):
    NJ = NCH // NB
    NBLK = NCH // 512
    assert NJ % 512 == 0 or NJ == NCH
    nc = bass.Bass("TRN2", target_bir_lowering=False)
    U = nc.dram_tensor("U", [128, GPC, NCH], F32, kind="ExternalInput").ap()
    pars = nc.dram_tensor("pars", [64, 3, GPC], F32, kind="ExternalInput").ap()
    Bri = nc.dram_tensor("Bri", [64, 2, GPC, 16], F32, kind="ExternalInput").ap()
    Cri = nc.dram_tensor("Cri", [64, 2, GPC, 16], F32, kind="ExternalInput").ap()
    dcol = nc.dram_tensor("dcol", [128, GPC], F32, kind="ExternalInput").ap()
    Y = nc.dram_tensor("Y", [128, GPC, NCH], F32, kind="ExternalOutput").ap()
    with ExitStack() as es:
        cx = Ctx(nc, es)
        G = GPC
        par = cx.sb("par", [64, 3, G], F32)
        B_ = cx.sb("B_", [64, 2, G, 16], F32)
        C_ = cx.sb("C_", [64, 2, G, 16], F32)
        dc = cx.sb("dc", [128, G], F32)
        sc = cx.sb("sc", [64, 24, G], F32)
        Pn = cx.sb("Pn", [64, 2, 8, G], F32)
        Pp = cx.sb("Pp", [64, 2, 16, G], F32)
        Bc = cx.sb("Bc", [64, 2, G, 16], F32)
        tb = cx.sb("tb", [64, 4, G, 16], F32)
        Bs = cx.sb("Bs", [64, 2, G, 128], BF16)
        Ct = cx.sb("Ct", [64, 2, G, 256], BF16)
        TT = cx.sb("TT", [128, G, 128], BF16)
        GT = cx.sb("GT", [128, G, 128], BF16)
        maskT = cx.sb("maskT", [128, 128], F32)
        identF = cx.sb("identF", [128, 128], F32)
        identB = cx.sb("identB", [128, 128], BF16)
        onesF = cx.sb("onesF", [128, 128], F32)
        tmpT = cx.sb("tmpT", [128, 2, 128], F32)
        cst = cx.sb("cst", [64, 2], F32)
        Ubf = cx.sb("Ubf", [128, G, NCH], BF16)
        ust = cx.sb("ust", [128, 2, NCH], F32)
        WS = cx.sb("WS", [64, 2, NB, G, NJ + 1], BF16)
        NS = 2 * NB * G
        S = cx.sb("S", [64, NS], F32)
        A8R = cx.sb("A8R", [64, NS], F32)
        A8I = cx.sb("A8I", [64, NS], F32)
        t1 = cx.sb("t1", [64, NS], F32)
        t2 = cx.sb("t2", [64, NS], F32)
        ost = cx.sb("ost", [128, 4, 512], F32)
        pT = cx.ps("pT", [128, 128])
        pG = cx.ps("pG", [128, 128])
        pW = [[cx.ps(f"pW{i}{h}", [64, 512]) for h in range(2)] for i in range(2)]
        pY = [cx.ps(f"pY{i}", [128, 512]) for i in range(2)]

        t_par = cx.dma("sp", par[:], pars[:, :, :], key="c0")
        t_B = cx.dma("sp", B_[:], Bri[:, :, :, :], key="c1")
        t_C = cx.dma("sp", C_[:], Cri[:, :, :, :], key="c2")
        t_d = cx.dma("sp", dc[:], dcol[:, :], key="c3")
        tp = cx.op("pool", lambda e: e.memset(onesF[:], 1.0))
        t_mask = cx.op("pool", lambda e: e.affine_select(
            out=maskT[:].rearrange("p (t c) -> p t c", c=16), in_=onesF[:].rearrange("p (t c) -> p t c", c=16),
            pattern=[[16, 8], [0, 16]], compare_op=ALU.is_ge, fill=0.0, base=15, channel_multiplier=-1), deps=[tp])
        t_idF = cx.op("pool", lambda e: e.affine_select(
            out=identF[:], in_=onesF[:], pattern=[[-1, 128]], compare_op=ALU.is_equal, fill=0.0, base=0,
            channel_multiplier=1), deps=[tp])
        t_idB = cx.op("pool", lambda e: e.tensor_copy(out=identB[:], in_=identF[:]), deps=[t_idF])
        cx.op("pool", lambda e: e.memset(cst[:, 0:1], float(np.pi / 2)))
        t_cst = cx.op("pool", lambda e: e.memset(cst[:, 1:2], 0.0))
        t_ws0 = cx.op("pool", lambda e: e.memset(WS[:, :, :, :, 0:1], 0.0))
        t_s0 = cx.op("pool", lambda e: e.memset(S[:], 0.0))

        t_u = []
        ust_free = [None, None]
        for gl in range(G):
            sl = gl % 2
            t_l = cx.dma("sp", ust[:, sl, :], U[:, gl, :], key=f"u{sl}", deps=[ust_free[sl]])
            t_c = cx.op("pool", lambda e: e.tensor_copy(out=Ubf[:, gl, :], in_=ust[:, sl, :]), deps=[t_l])
            ust_free[sl] = t_c
            t_u.append(t_c)

        last = {"t": [t_par, t_cst]}

        def V(fn, extra=()):
            last["t"] = cx.op("dve", fn, deps=[last["t"]] + list(extra))
            return last["t"]

        def A(fn, extra=()):
            last["t"] = cx.op("act", fn, deps=[last["t"]] + list(extra))
            return last["t"]

        def s_(i):
            return sc[:, i, :]
        ar, ai, ldt = par[:, 0, :], par[:, 1, :], par[:, 2, :]
        DT, X1, MAG, TH, SN, CS = range(6)
        A(lambda e: e.activation(out=s_(DT), in_=ldt, func=AF.Exp))
        V(lambda e: e.tensor_tensor(out=s_(X1), in0=ar, in1=s_(DT), op=ALU.mult))
        A(lambda e: e.activation(out=s_(MAG), in_=s_(X1), func=AF.Exp, scale=1.0 / 16))
        V(lambda e: e.tensor_tensor(out=s_(TH), in0=ai, in1=s_(DT), op=ALU.mult))
        A(lambda e: e.activation(out=s_(SN), in_=s_(TH), func=AF.Sin, scale=1.0 / 16))
        A(lambda e: e.activation(out=s_(CS), in_=s_(TH), func=AF.Sin, scale=-1.0 / 16, bias=cst[:, 0:1]))
        R0, I0, R1, I1, TA, TB = 6, 7, 8, 9, 10, 11
        V(lambda e: e.tensor_tensor(out=s_(R0), in0=s_(MAG), in1=s_(CS), op=ALU.mult))
        V(lambda e: e.tensor_tensor(out=s_(I0), in0=s_(MAG), in1=s_(SN), op=ALU.mult))

        def cmul(oR, oI, aR, aI, bR, bI):
            V(lambda e: e.tensor_tensor(out=s_(TA), in0=aR, in1=bR, op=ALU.mult))
            V(lambda e: e.tensor_tensor(out=s_(TB), in0=aI, in1=bI, op=ALU.mult))
            V(lambda e: e.tensor_tensor(out=oR, in0=s_(TA), in1=s_(TB), op=ALU.subtract))
            V(lambda e: e.tensor_tensor(out=s_(TA), in0=aR, in1=bI, op=ALU.mult))
            V(lambda e: e.tensor_tensor(out=s_(TB), in0=aI, in1=bR, op=ALU.mult))
            V(lambda e: e.tensor_tensor(out=oI, in0=s_(TA), in1=s_(TB), op=ALU.add))
        cur = (R0, I0)
        nxt = (R1, I1)
        for _ in range(4):
            cmul(s_(nxt[0]), s_(nxt[1]), s_(cur[0]), s_(cur[1]), s_(cur[0]), s_(cur[1]))
            cur, nxt = nxt, cur
        AR, AI = cur
        DEN, NR, CR, CI, M2 = 12, 13, 14, 15, 16
        V(lambda e: e.tensor_tensor(out=s_(TA), in0=ar, in1=ar, op=ALU.mult))
        V(lambda e: e.tensor_tensor(out=s_(TB), in0=ai, in1=ai, op=ALU.mult))
        V(lambda e: e.tensor_tensor(out=s_(DEN), in0=s_(TA), in1=s_(TB), op=ALU.add))
        V(lambda e: e.reciprocal(out=s_(DEN), in_=s_(DEN)))
        V(lambda e: e.tensor_scalar(out=s_(NR), in0=s_(AR), scalar1=-1.0, scalar2=None, op0=ALU.add))
        V(lambda e: e.tensor_tensor(out=s_(TA), in0=s_(NR), in1=ar, op=ALU.mult))
        V(lambda e: e.tensor_tensor(out=s_(TB), in0=s_(AI), in1=ai, op=ALU.mult))
        V(lambda e: e.tensor_tensor(out=s_(TA), in0=s_(TA), in1=s_(TB), op=ALU.add))
        V(lambda e: e.tensor_tensor(out=s_(CR), in0=s_(TA), in1=s_(DEN), op=ALU.mult))
        V(lambda e: e.tensor_tensor(out=s_(TA), in0=s_(AI), in1=ar, op=ALU.mult))
        V(lambda e: e.tensor_tensor(out=s_(TB), in0=s_(NR), in1=ai, op=ALU.mult))
        V(lambda e: e.tensor_tensor(out=s_(TA), in0=s_(TA), in1=s_(TB), op=ALU.subtract))
        V(lambda e: e.tensor_tensor(out=s_(CI), in0=s_(TA), in1=s_(DEN), op=ALU.mult))
        V(lambda e: e.tensor_copy(out=Pp[:, 0, 0, :], in_=onesF[0:64, 0:G]), [tp])
        V(lambda e: e.tensor_copy(out=Pn[:, 0, 0, :], in_=onesF[0:64, 0:G]))
        V(lambda e: e.memset(Pp[:, 1, 0, :], 0.0))
        V(lambda e: e.memset(Pn[:, 1, 0, :], 0.0))
        V(lambda e: e.tensor_copy(out=Pp[:, 0, 1, :], in_=s_(AR)))
        V(lambda e: e.tensor_copy(out=Pp[:, 1, 1, :], in_=s_(AI)))
        V(lambda e: e.tensor_tensor(out=s_(TA), in0=s_(AR), in1=s_(AR), op=ALU.mult))
        V(lambda e: e.tensor_tensor(out=s_(TB), in0=s_(AI), in1=s_(AI), op=ALU.mult))
        V(lambda e: e.tensor_tensor(out=s_(M2), in0=s_(TA), in1=s_(TB), op=ALU.add))
        V(lambda e: e.reciprocal(out=s_(M2), in_=s_(M2)))
        V(lambda e: e.tensor_tensor(out=Pn[:, 0, 1, :], in0=s_(AR), in1=s_(M2), op=ALU.mult))
        V(lambda e: e.scalar_tensor_tensor(out=Pn[:, 1, 1, :], in0=s_(AI), scalar=-1.0, in1=s_(M2),
                                           op0=ALU.mult, op1=ALU.mult))
        for k in range(2, 16):
            cmul(Pp[:, 0, k, :], Pp[:, 1, k, :], Pp[:, 0, k - 1, :], Pp[:, 1, k - 1, :], Pp[:, 0, 1, :], Pp[:, 1, 1, :])
        for k in range(2, 8):
            cmul(Pn[:, 0, k, :], Pn[:, 1, k, :], Pn[:, 0, k - 1, :], Pn[:, 1, k - 1, :], Pn[:, 0, 1, :], Pn[:, 1, 1, :])
        for r in range(2 * NB):
            V(lambda e: e.tensor_copy(out=A8R[:, r * G:(r + 1) * G], in_=Pp[:, 0, 8, :]))
        for r in range(NB):
            V(lambda e: e.tensor_scalar(out=A8I[:, r * G:(r + 1) * G], in0=Pp[:, 1, 8, :], scalar1=-1.0, scalar2=None,
                                        op0=ALU.mult))
            V(lambda e: e.tensor_copy(out=A8I[:, (NB + r) * G:(NB + r + 1) * G], in_=Pp[:, 1, 8, :]))

        def bc(ap2d):
            return ap2d.unsqueeze(2).to_broadcast([64, G, 16])
        V(lambda e: e.tensor_tensor(out=tb[:, 0], in0=B_[:, 0], in1=bc(s_(CR)), op=ALU.mult), [t_B])
        V(lambda e: e.tensor_tensor(out=tb[:, 1], in0=B_[:, 1], in1=bc(s_(CI)), op=ALU.mult))
        V(lambda e: e.tensor_tensor(out=Bc[:, 0], in0=tb[:, 0], in1=tb[:, 1], op=ALU.subtract))
        V(lambda e: e.tensor_tensor(out=tb[:, 0], in0=B_[:, 1], in1=bc(s_(CR)), op=ALU.mult))
        V(lambda e: e.tensor_tensor(out=tb[:, 1], in0=B_[:, 0], in1=bc(s_(CI)), op=ALU.mult))
        V(lambda e: e.tensor_tensor(out=Bc[:, 1], in0=tb[:, 0], in1=tb[:, 1], op=ALU.add))
        BsV = [Bs[:, h].rearrange("p g (s c) -> p g s c", c=16) for h in range(2)]
        CtV = [Ct[:, h].rearrange("p g (t c) -> p g t c", c=16) for h in range(2)]
        for s in range(8):
            pr, pi = bc(Pn[:, 0, s, :]), bc(Pn[:, 1, s, :])
            V(lambda e: e.tensor_tensor(out=tb[:, 0], in0=Bc[:, 0], in1=pr, op=ALU.mult))
            V(lambda e: e.tensor_tensor(out=tb[:, 1], in0=Bc[:, 1], in1=pi, op=ALU.mult))
            V(lambda e: e.tensor_tensor(out=BsV[0][:, :, s, :], in0=tb[:, 0], in1=tb[:, 1], op=ALU.subtract))
            V(lambda e: e.tensor_tensor(out=tb[:, 0], in0=Bc[:, 0], in1=pi, op=ALU.mult))
            V(lambda e: e.tensor_tensor(out=tb[:, 1], in0=Bc[:, 1], in1=pr, op=ALU.mult))
            V(lambda e: e.tensor_tensor(out=BsV[1][:, :, s, :], in0=tb[:, 0], in1=tb[:, 1], op=ALU.add))
        t_bs = last["t"]
        for t in range(16):
            pr, pi = bc(Pp[:, 0, t, :]), bc(Pp[:, 1, t, :])
            V(lambda e: e.tensor_tensor(out=tb[:, 0], in0=C_[:, 0], in1=pr, op=ALU.mult), [t_C])
            V(lambda e: e.tensor_tensor(out=tb[:, 1], in0=C_[:, 1], in1=pi, op=ALU.mult))
            V(lambda e: e.tensor_tensor(out=CtV[0][:, :, t, :], in0=tb[:, 0], in1=tb[:, 1], op=ALU.subtract))
            V(lambda e: e.tensor_tensor(out=tb[:, 0], in0=C_[:, 0], in1=pi, op=ALU.mult))
            V(lambda e: e.tensor_tensor(out=tb[:, 1], in0=C_[:, 1], in1=pr, op=ALU.mult))
            V(lambda e: e.scalar_tensor_tensor(out=CtV[1][:, :, t, :], in0=tb[:, 0], scalar=-1.0, in1=tb[:, 1],
                                               op0=ALU.mult, op1=ALU.subtract))
        t_ct = last["t"]

        t_TT, t_GT = [], []
        pT_free = None
        pG_free = None
        for gl in range(G):
            cx.op("pe", lambda e: e.matmul(pT[:, :], lhsT=Bs[:, 0, gl, :], rhs=Ct[:, 0, gl, 0:128], start=True, stop=False),
                  deps=[t_bs, t_ct, pT_free])
            t_m = cx.op("pe", lambda e: e.matmul(pT[:, :], lhsT=Bs[:, 1, gl, :], rhs=Ct[:, 1, gl, 0:128], start=False, stop=True))
            sl = gl % 2
            t_a = cx.op("dve", lambda e: e.tensor_tensor(out=tmpT[:, sl, :], in0=pT[:, :], in1=maskT[:], op=ALU.mult),
                        deps=[t_m, t_mask, last["t"]])
            pT_free = t_a
            t_b = cx.op("dve", lambda e: e.scalar_tensor_tensor(out=TT[:, gl, :], in0=identF[:], scalar=dc[:, gl:gl + 1],
                                                                in1=tmpT[:, sl, :], op0=ALU.mult, op1=ALU.add),
                        deps=[t_a, t_idF, t_d])
            last["t"] = t_b
            t_TT.append(t_b)
            cx.op("pe", lambda e: e.matmul(pG[:, 0:64], lhsT=Bs[:, 0, gl, :], rhs=identB[0:64, 0:64], start=True, stop=True),
                  deps=[t_idB, pG_free])
            t_m2 = cx.op("pe", lambda e: e.matmul(pG[:, 64:128], lhsT=Bs[:, 1, gl, :], rhs=identB[0:64, 0:64], start=True, stop=True))
            t_g = cx.op("act", lambda e: e.activation(out=GT[:, gl, :], in_=pG[:, :], func=AF.Copy), deps=[t_m2])
            pG_free = t_g
            t_GT.append(t_g)

        pW_free = [[None, None], [None, None]]
        t_W = None
        k = 0
        for gl in range(G):
            for blk in range(NBLK):
                i = k % 2
                k += 1
                b, j0 = (blk * 512) // NJ, (blk * 512) % NJ
                for h in range(2):
                    t_m = cx.op("pe", lambda e: e.matmul(pW[i][h][:, :], lhsT=GT[:, gl, h * 64:(h + 1) * 64],
                                                         rhs=Ubf[:, gl, blk * 512:(blk + 1) * 512], start=True, stop=True),
                                deps=[t_GT[gl], t_u[gl], pW_free[i][h]])
                    dst = WS[:, h, b, gl, 1 + j0:1 + j0 + 512]
                    if h == 0:
                        t_e = cx.op("act", lambda e: e.activation(out=dst, in_=pW[i][h][:, :], func=AF.Copy), deps=[t_m])
                    else:
                        t_e = cx.op("pool", lambda e: e.tensor_copy(out=dst, in_=pW[i][h][:, :]), deps=[t_m]) if False else \
                            cx.op("act", lambda e: e.activation(out=dst, in_=pW[i][h][:, :], func=AF.Copy), deps=[t_m])
                    pW_free[i][h] = t_e
                    t_W = t_e

        Sv = S[:].rearrange("p (h x) -> p h x", h=2)
        t_prev = [t_W, t_s0, t_ws0, last["t"]]
        st_tok = None
        for j in range(NJ):
            wj = WS[:, :, :, :, j + 1]
            ta = cx.op("dve", lambda e: e.tensor_tensor(out=t1[:], in0=S[:], in1=A8R[:], op=ALU.mult), deps=[t_prev])
            tb_ = cx.op("dve", lambda e: e.tensor_tensor(out=t2[:].rearrange("p (h x) -> p h x", h=2), in0=Sv[:, ::-1, :],
                                                         in1=A8I[:].rearrange("p (h x) -> p h x", h=2), op=ALU.mult))
            tc = cx.op("dve", lambda e: e.tensor_tensor(out=t2[:].rearrange("p (a b c) -> p a b c", a=2, b=NB), in0=t2[:].rearrange("p (a b c) -> p a b c", a=2, b=NB),
                                                        in1=wj, op=ALU.add), deps=[tb_])
            td = cx.op("dve", lambda e: e.tensor_tensor(out=S[:], in0=t1[:], in1=t2[:], op=ALU.add), deps=[tc, st_tok])
            if j < NJ - 1:
                st_tok = cx.op("act", lambda e: e.activation(out=wj, in_=S[:].rearrange("p (a b c) -> p a b c", a=2, b=NB),
                                                             func=AF.Copy), deps=[td])
            t_prev = [td]

        ost_free = [None] * 4
        pY_free = [None, None]
        outs = []
        k = 0
        for gl in range(G):
            for blk in range(NBLK):
                i = k % 2
                sl = k % 4
                k += 1
                b, j0 = (blk * 512) // NJ, (blk * 512) % NJ
                cx.op("pe", lambda e: e.matmul(pY[i][:, :], lhsT=TT[:, gl, :], rhs=Ubf[:, gl, blk * 512:(blk + 1) * 512],
                                               start=True, stop=False), deps=[t_TT[gl], pY_free[i], st_tok])
                cx.op("pe", lambda e: e.matmul(pY[i][:, :], lhsT=Ct[:, 0, gl, 128:256], rhs=WS[:, 0, b, gl, j0:j0 + 512],
                                               start=False, stop=False))
                t_m = cx.op("pe", lambda e: e.matmul(pY[i][:, :], lhsT=Ct[:, 1, gl, 128:256], rhs=WS[:, 1, b, gl, j0:j0 + 512],
                                                     start=False, stop=True))
                t_e = cx.op("act", lambda e: e.activation(out=ost[:, sl, :], in_=pY[i][:, :], func=AF.Gelu_apprx_tanh),
                            deps=[t_m, ost_free[sl]])
                pY_free[i] = t_e
                t_o = cx.dma("sp", Y[:, gl, blk * 512:(blk + 1) * 512], ost[:, sl, :], key=f"y{sl}", deps=[t_e])
                ost_free[sl] = t_o
                outs.append(t_o)
        cx.wait("sp", outs)
    return nc


def s5_host_inputs(u3, a_re, a_im, log_dt, b_re, b_im, c_re, c_im, d, g0, GPC):
    NB, S, _ = u3.shape
    gs = slice(g0, g0 + GPC)
    U = u3.reshape(NB, S // 8, 8, 128, 16)[:, :, :, gs, :].transpose(2, 4, 3, 0, 1).reshape(128, GPC, NB * (S // 8))
    pars = np.stack([a_re[gs].T, a_im[gs].T, np.broadcast_to(log_dt[gs][None, :], (64, GPC))], axis=1)
    Bri = np.stack([b_re[gs].transpose(1, 0, 2), b_im[gs].transpose(1, 0, 2)], axis=1)
    Cri = np.stack([c_re[gs].transpose(2, 0, 1), c_im[gs].transpose(2, 0, 1)], axis=1)
    dcol = np.tile(d.reshape(128, 16)[gs].T, (8, 1))
    f = lambda a: np.ascontiguousarray(a, dtype=np.float32)
    return {"U": f(U), "pars": f(pars), "Bri": f(Bri), "Cri": f(Cri), "dcol": f(dcol)}


def run_s5(u3, a_re, a_im, log_dt, b_re, b_im, c_re, c_im, d):
    NB, S, _ = u3.shape
    GPC = 128 // NCORES
    nc = build_s5(GPC, NB * S // 8, NB)
    ins = [s5_host_inputs(u3, a_re, a_im, log_dt, b_re, b_im, c_re, c_im, d, c * GPC, GPC) for c in range(NCORES)]
    res = run_bass_kernel_spmd(nc, ins, core_ids=list(range(NCORES)))
    Yall = np.concatenate([r["Y"] for r in res.results], axis=1)
    y = Yall.reshape(8, 16, 128, NB, S // 8).transpose(3, 4, 0, 2, 1).reshape(NB, S, 2048)
    return y


DH = 128
ATT_SCALE = DH ** -0.5


def build_attn(NH, S, qchunks):
    NQC = len(qchunks)
    NKT = S // 128
    nc = bass.Bass("TRN2", target_bir_lowering=False)
    QT = nc.dram_tensor("QT", [NH, DH, NQC * 512], F32, kind="ExternalInput").ap()
    KT = nc.dram_tensor("KT", [NH, DH, S], F32, kind="ExternalInput").ap()
    Vd = nc.dram_tensor("V", [NH, S, DH], F32, kind="ExternalInput").ap()
    FL = nc.dram_tensor("FL", [128, NH, NKT], F32, kind="ExternalInput").ap()
    BFd = nc.dram_tensor("BF", [128, NH], F32, kind="ExternalInput").ap()
    OT = nc.dram_tensor("OT", [NH, DH, NQC * 512], F32, kind="ExternalOutput").ap()
    with ExitStack() as es:
        cx = Ctx(nc, es)
        fl = cx.sb("fl", [128, NH, NKT], F32)
        bfs = cx.sb("bfs", [128, NH], F32)
        Lc = cx.sb("Lc", [128, NH, NKT], F32)
        CL = cx.sb("CL", [128, NH, NKT], F32)
        Tot = cx.sb("Tot", [128, NH, NKT], F32)
        Exc = cx.sb("Exc", [128, NH, NKT], F32)
        Fac = cx.sb("Fac", [128, NH, NKT], F32)
        Bias = cx.sb("Bias", [128, NH, NKT, NKT], F32)
        onesF = cx.sb("onesF", [128, 128], F32)
        triF = cx.sb("triF", [128, 128], F32)
        triB = cx.sb("triB", [128, 128], BF16)
        onesB = cx.sb("onesB", [128, 128], BF16)
        one1 = cx.sb("one1", [128, 1], F32)
        qst = cx.sb("qst", [128, NQC * 512], F32)
        kst = cx.sb("kst", [128, S], F32)
        vst = cx.sb("vst", [128, NKT, DH], F32)
        Qb = cx.sb("Qb", [128, 2, NQC * 512], BF16)
        Kb = cx.sb("Kb", [128, 2, S], BF16)
        Vb = cx.sb("Vb", [128, 2, NKT, DH], BF16)
        Pt = cx.sb("Pt", [128, 3, 512], BF16)
        Od = cx.sb("Od", [128, 2, 512], F32)
        ofin = cx.sb("ofin", [128, 2, 512], F32)
        dfin = cx.sb("dfin", [128, 2, 512], F32)
        pS = [cx.ps(f"pS{i}", [128, 512]) for i in range(2)]
        pOo = cx.ps("pOo", [128, 512])
        pDo = cx.ps("pDo", [128, 512])
        pOd = cx.ps("pOd", [128, 512])
        pDd = cx.ps("pDd", [128, 512])
        pC = cx.ps("pC", [128, 2, NH * NKT])

        tp = cx.op("pool", lambda e: e.memset(onesF[:], 1.0))
        t_tri = cx.op("pool", lambda e: e.affine_select(out=triF[:], in_=onesF[:], pattern=[[1, 128]], compare_op=ALU.is_ge,
                                                        fill=0.0, base=0, channel_multiplier=-1), deps=[tp])
        t_triB = cx.op("pool", lambda e: e.tensor_copy(out=triB[:], in_=triF[:]), deps=[t_tri])
        t_oB = cx.op("pool", lambda e: e.tensor_copy(out=onesB[:], in_=onesF[:]), deps=[tp])
        t_o1 = cx.op("pool", lambda e: e.memset(one1[:], 1.0))
        t_fl = cx.dma("sp", fl[:], FL[:, :, :], key="c0")
        t_bf = cx.dma("sp", bfs[:], BFd[:, :], key="c1")
        t = None
        for h in range(NH):
            t = cx.op("dve", lambda e: e.tensor_scalar(out=Lc[:, h, :], in0=fl[:, h, :], scalar1=bfs[:, h:h + 1], scalar2=None,
                                                       op0=ALU.add), deps=[t_fl, t_bf])
        t = cx.op("act", lambda e: e.activation(out=Lc[:], in_=Lc[:], func=AF.Exp, scale=-1.0), deps=[t])
        t = cx.op("act", lambda e: e.activation(out=Lc[:], in_=Lc[:], func=AF.Ln, bias=one1[:, 0:1]), deps=[t, t_o1])
        Lf = Lc[:].rearrange("p h k -> p (h k)")
        cx.op("pe", lambda e: e.matmul(pC[:, 0, :], lhsT=triF[:], rhs=Lf, start=True, stop=True), deps=[t, t_tri])
        t_mm = cx.op("pe", lambda e: e.matmul(pC[:, 1, :], lhsT=onesF[:], rhs=Lf, start=True, stop=True))
        t = cx.op("dve", lambda e: e.tensor_copy(out=Tot[:].rearrange("p h k -> p (h k)"), in_=pC[:, 1, :]), deps=[t_mm])
        for h in range(NH):
            t = cx.op("dve", lambda e: e.tensor_tensor_scan(out=Exc[:, h, :], data0=onesF[:, 0:NKT], data1=Tot[:, h, :],
                                                            initial=0.0, op0=ALU.mult, op1=ALU.add), deps=[t])
        t = cx.op("dve", lambda e: e.tensor_tensor(out=Exc[:], in0=Exc[:], in1=Tot[:], op=ALU.subtract), deps=[t])
        t = cx.op("dve", lambda e: e.tensor_tensor(out=CL[:].rearrange("p h k -> p (h k)"), in0=pC[:, 0, :],
                                                   in1=Exc[:].rearrange("p h k -> p (h k)"), op=ALU.add), deps=[t])
        for h in range(NH):
            for ref in range(NKT):
                t = cx.op("dve", lambda e: e.tensor_scalar(out=Bias[:, h, ref, :], in0=CL[:, h, :], scalar1=Exc[:, h, ref:ref + 1],
                                                           scalar2=None, op0=ALU.subtract), deps=[t] if ref == 0 and h == 0 else [])
        t_bias = t
        for h in range(NH):
            for c in range(NKT // 4):
                t = cx.op("dve", lambda e: e.tensor_scalar(out=Fac[:, h, 4 * c:4 * c + 4], in0=Exc[:, h, 4 * c:4 * c + 4],
                                                           scalar1=Exc[:, h, 4 * c:4 * c + 1], scalar2=-1.0,
                                                           op0=ALU.subtract, op1=ALU.mult))
        t_fac = cx.op("act", lambda e: e.activation(out=Fac[:], in_=Fac[:], func=AF.Exp), deps=[t])

        cast_tok = {"q": None, "k": None, "v": None}
        head_done = [None, None]
        pS_free = [None, None]
        Pt_free = [None, None, None]
        fin_free = [None, None]
        acc_free = {"Oo": None, "Do": None, "Od": None, "Dd": None}
        outs = []
        gstep = 0
        for h in range(NH):
            hb = h % 2
            tq = cx.dma("sp", qst[:], QT[h, :, :], key="lq", deps=[cast_tok["q"]])
            tk = cx.dma("sp", kst[:], KT[h, :, :], key="lk", deps=[cast_tok["k"]])
            tv = cx.dma("sp", vst[:], Vd[h].rearrange("(kt p) d -> p kt d", p=128), key="lv", deps=[cast_tok["v"]])
            cast_tok["q"] = cx.op("pool", lambda e: e.tensor_copy(out=Qb[:, hb, :], in_=qst[:]), deps=[tq, head_done[hb]])
            cast_tok["k"] = cx.op("pool", lambda e: e.tensor_copy(out=Kb[:, hb, :], in_=kst[:]), deps=[tk, head_done[hb]])
            cast_tok["v"] = cx.op("pool", lambda e: e.tensor_copy(out=Vb[:, hb], in_=vst[:]), deps=[tv, head_done[hb]])
            ld_deps = [cast_tok["q"], cast_tok["k"], cast_tok["v"]]
            for qi, qc in enumerate(qchunks):
                steps = []
                for kt in range(4 * qc):
                    steps.append(dict(kt=kt, q0=qi * 512, n=512, c0=0, bias=Bias[:, h, 4 * qc, kt:kt + 1], mask=False,
                                      accO=pOo, accD=pDo, start=(kt == 0), stop=(kt == 4 * qc - 1), kind="o"))
                for rp in range(4):
                    for r in range(rp + 1):
                        kt = 4 * qc + r
                        steps.append(dict(kt=kt, q0=qi * 512 + rp * 128, n=128, c0=rp * 128,
                                          bias=Bias[:, h, 4 * qc + rp, kt:kt + 1], mask=(r == rp),
                                          accO=pOd, accD=pDd, start=(r == 0), stop=(r == rp), kind="d"))
                n = len(steps)
                exp_tok = [None] * n
                last_acc = {"o": None, "d": None}
                first_acc_wait = {"o": [acc_free["Oo"], acc_free["Do"]], "d": [acc_free["Od"], acc_free["Dd"]]}
                for i in range(n + 1):
                    if i < n:
                        st = steps[i]
                        bi = (gstep + i) % 2
                        sl = (gstep + i) % 3
                        t_qk = cx.op("pe", lambda e: e.matmul(pS[bi][:, 0:st["n"]], lhsT=Kb[:, hb, st["kt"] * 128:(st["kt"] + 1) * 128],
                                                              rhs=Qb[:, hb, st["q0"]:st["q0"] + st["n"]], start=True, stop=True),
                                     deps=[pS_free[bi], ld_deps])
                        t_e = cx.op("act", lambda e: e.activation(out=Pt[:, sl, 0:st["n"]], in_=pS[bi][:, 0:st["n"]], func=AF.Exp,
                                                                  scale=ATT_SCALE, bias=st["bias"]),
                                    deps=[t_qk, Pt_free[sl], t_bias])
                        pS_free[bi] = t_e
                        if st["mask"]:
                            t_e = cx.op("dve", lambda e: e.tensor_tensor(out=Pt[:, sl, 0:128], in0=Pt[:, sl, 0:128], in1=triB[:],
                                                                         op=ALU.mult), deps=[t_e, t_triB])
                        exp_tok[i] = (t_e, sl)
                    if i >= 1:
                        st = steps[i - 1]
                        t_e, sl = exp_tok[i - 1]
                        c0, nn = st["c0"], st["n"]
                        fw = first_acc_wait[st["kind"]] if st["start"] and c0 == 0 else []
                        cx.op("pe", lambda e: e.matmul(st["accO"][:, c0:c0 + nn], lhsT=Vb[:, hb, st["kt"], :], rhs=Pt[:, sl, 0:nn],
                                                       start=st["start"], stop=st["stop"]), deps=[t_e] + fw)
                        t_pv = cx.op("pe", lambda e: e.matmul(st["accD"][:, c0:c0 + nn], lhsT=onesB[:], rhs=Pt[:, sl, 0:nn],
                                                              start=st["start"], stop=st["stop"]), deps=[t_oB])
                        Pt_free[sl] = t_pv
                        last_acc[st["kind"]] = t_pv
                gstep += n
                fs = (h * NQC + qi) % 2
                t_c1 = cx.op("act", lambda e: e.activation(out=Od[:, 0, :], in_=pOd[:, :], func=AF.Copy),
                             deps=[last_acc["d"], fin_free[fs]] + outs[-2:])
                t_c2 = cx.op("act", lambda e: e.activation(out=Od[:, 1, :], in_=pDd[:, :], func=AF.Copy))
                acc_free["Od"] = acc_free["Dd"] = t_c2
                if qc > 0:
                    t_a = None
                    for rp in range(4):
                        cs = slice(rp * 128, (rp + 1) * 128)
                        fc = Fac[:, h, 4 * qc + rp:4 * qc + rp + 1]
                        cx.op("dve", lambda e: e.scalar_tensor_tensor(out=ofin[:, fs, cs], in0=pOo[:, cs], scalar=fc, in1=Od[:, 0, cs],
                                                                      op0=ALU.mult, op1=ALU.add), deps=[t_c2, last_acc["o"], t_fac])
                        t_a = cx.op("dve", lambda e: e.scalar_tensor_tensor(out=dfin[:, fs, cs], in0=pDo[:, cs], scalar=fc,
                                                                            in1=Od[:, 1, cs], op0=ALU.mult, op1=ALU.add))
                    acc_free["Oo"] = acc_free["Do"] = t_a
                    t_r = cx.op("dve", lambda e: e.reciprocal(out=dfin[:, fs, :], in_=dfin[:, fs, :]), deps=[t_a])
                    t_f = cx.op("dve", lambda e: e.tensor_tensor(out=ofin[:, fs, :], in0=ofin[:, fs, :], in1=dfin[:, fs, :],
                                                                 op=ALU.mult), deps=[t_r])
                else:
                    t_r = cx.op("dve", lambda e: e.reciprocal(out=dfin[:, fs, :], in_=Od[:, 1, :]), deps=[t_c2])
                    t_f = cx.op("dve", lambda e: e.tensor_tensor(out=ofin[:, fs, :], in0=Od[:, 0, :], in1=dfin[:, fs, :],
                                                                 op=ALU.mult), deps=[t_r])
                t_o = cx.dma("sp", OT[h, :, qi * 512:(qi + 1) * 512], ofin[:, fs, :], key=f"o{fs}", deps=[t_f])
                fin_free[fs] = t_o
                outs.append(t_f)
                outs_dma = t_o
            head_done[hb] = cx.op("pe", lambda e: e.matmul(pS[0][0:1, 0:1], lhsT=onesB[0:1, 0:1], rhs=onesB[0:1, 0:1],
                                                           start=True, stop=True), deps=[pS_free[0]])
            pS_free[0] = head_done[hb]
        cx.wait("sp", [fin_free[0], fin_free[1]])
    return nc


def run_attn(q4, k4, v4, f_logit, b_f):
    B, H, S, dh = q4.shape
    NH = B * H // NCORES
    nc = build_attn(NH, S, list(range(S // 512)))
    ins = []
    f = lambda a: np.ascontiguousarray(a, dtype=np.float32)
    for c in range(NCORES):
        b, h0 = (c * NH) // H, (c * NH) % H
        hs = slice(h0, h0 + NH)
        FL = f_logit[b][:, hs].reshape(S // 128, 128, NH).transpose(1, 2, 0)
        ins.append({"QT": f(q4[b, hs].transpose(0, 2, 1)), "KT": f(k4[b, hs].transpose(0, 2, 1)), "V": f(v4[b, hs]),
                    "FL": f(FL), "BF": f(np.broadcast_to(b_f[hs][None, :], (128, NH)))})
    res = run_bass_kernel_spmd(nc, ins, core_ids=list(range(NCORES)))
    o = np.zeros((B, H, S, dh), np.float32)
    for c in range(NCORES):
        b, h0 = (c * NH) // H, (c * NH) % H
        o[b, h0:h0 + NH] = res.results[c]["OT"].transpose(0, 2, 1)
    return o


def kernel(x, norm_pre, norm_post, s5_w_in, s5_a_re, s5_a_im, s5_log_dt, s5_b_re, s5_b_im,
           s5_c_re, s5_c_im, s5_d, s5_w_glu, s5_b_glu, s5_w_out, kv_norm, kv_w, kv_b_f,
           fox_w_in, fox_w_out):
    f = lambda a: np.asarray(a, dtype=np.float32)
    x = f(x)
    B, S, _ = x.shape
    H = 16
    x2d = x.reshape(B * S, D)
    uz = run_norm_linear(x2d, f(norm_pre)[0], f(s5_w_in)[0])
    u, z = uz[:, :D], uz[:, D:]
    yg = run_s5(u.reshape(B, S, D), f(s5_a_re)[0], f(s5_a_im)[0], f(s5_log_dt)[0], f(s5_b_re)[0], f(s5_b_im)[0],
                f(s5_c_re)[0], f(s5_c_im)[0], f(s5_d)[0])
    h1 = run_gate_out(yg.reshape(B * S, D), z, x2d, f(norm_post)[0], f(s5_w_out)[0], f(s5_w_glu)[0], f(s5_b_glu)[0])
    kvf = run_norm_linear(h1, f(kv_norm), f(kv_w))
    qz = run_norm_linear(h1, f(norm_pre)[1], f(fox_w_in)[0])
    to4 = lambda a: a.reshape(B, S, H, DH).transpose(0, 2, 1, 3)
    o4 = run_attn(to4(qz[:, :D]), to4(kvf[:, :D]), to4(kvf[:, D:2 * D]), kvf[:, 2 * D:].reshape(B, S, H), f(kv_b_f))
    o2d = o4.transpose(0, 2, 1, 3).reshape(B * S, D)
    out = run_gate_out(o2d, qz[:, D:], h1, f(norm_post)[1], f(fox_w_out)[0])
    return out.reshape(B, S, D).astype(np.float32)
```

```python
import numpy as np
from contextlib import ExitStack
import concourse.bass as bass
import concourse.mybir as mybir
from concourse.bass_utils import run_bass_kernel_spmd

F32 = mybir.dt.float32
BF16 = mybir.dt.bfloat16
AF = mybir.ActivationFunctionType
ALU = mybir.AluOpType

NCORES = 8
D = 2048
TOK = 1024
RMS_EPS = 1e-6


class Ctx:
    def __init__(self, nc, es):
        self.nc = nc
        self.es = es
        self.eng = {"pe": nc.tensor, "act": nc.scalar, "dve": nc.vector,
                    "pool": nc.gpsimd, "sp": nc.sync}
        self.psem = {}
        self.cnt = {}
        for e in ("pe", "act", "dve", "pool"):
            self.psem[e] = es.enter_context(nc.semaphore("p_" + e))
            self.cnt[e] = 0
        self.waited = {e: {} for e in self.eng}
        self.dsem = {}
        self.cur = es
        self.ccn = 0
        self.cctoks = []
        self.uid = 0

    def scope(self):
        st = ExitStack()
        self.cur = st
        return st

    def sb(self, name, shape, dt):
        self.uid += 1
        return self.cur.enter_context(self.nc.sbuf_tensor(f"{name}_{self.uid}", shape, dt))

    def ps(self, name, shape, dt=F32):
        self.uid += 1
        return self.cur.enter_context(self.nc.psum_tensor(f"{name}_{self.uid}", shape, dt))

    def all_tokens(self):
        toks = [(self.psem[e], self.cnt[e], "p_" + e) for e in self.psem if self.cnt[e] > 0]
        toks += [(s[0], s[1], "d_" + k) for k, s in self.dsem.items() if s[1] > 0]
        toks += self.cctoks
        return toks

    def dma_tokens(self):
        return [(s[0], s[1], "d_" + k) for k, s in self.dsem.items() if s[1] > 0]

    def barrier(self):
        toks = [t for t in self.all_tokens() if t not in self.cctoks]
        for e in self.eng:
            self.wait(e, toks)

    def cc(self, kind, groups, in_ap, out_ap, deps=()):
        self.wait("pool", deps)
        sem = self.es.enter_context(self.nc.semaphore(f"cc{self.ccn}"))
        ins = self.nc.gpsimd.collective_compute(kind, ALU.bypass, replica_groups=groups, ins=[in_ap], outs=[out_ap])
        ins.then_inc(sem)
        tok = (sem, 1, f"cc{self.ccn}")
        self.ccn += 1
        self.cctoks.append(tok)
        return tok

    def idma(self, out, in_, idx, key, deps=()):
        self.wait("pool", deps)
        if key not in self.dsem:
            self.dsem[key] = [self.es.enter_context(self.nc.semaphore("d_" + key)), 0]
        s = self.dsem[key]
        ins = self.nc.gpsimd.indirect_dma_start(out=out, out_offset=None, in_=in_,
                                                in_offset=bass.IndirectOffsetOnAxis(ap=idx, axis=0))
        s[1] += 16
        ins.then_inc(s[0], 16)
        return (s[0], s[1], "d_" + key)

    def wait(self, e, deps):
        for t in deps:
            if t is None:
                continue
            if isinstance(t, list):
                self.wait(e, t)
                continue
            sem, val, key = t
            if self.waited[e].get(key, 0) >= val:
                continue
            self.eng[e].wait_ge(sem, val)
            self.waited[e][key] = val

    def op(self, e, fn, deps=()):
        self.wait(e, deps)
        ins = fn(self.eng[e])
        self.cnt[e] += 1
        ins.then_inc(self.psem[e], 1)
        return (self.psem[e], self.cnt[e], "p_" + e)

    def dma(self, q, out, in_, key, deps=(), **kw):
        self.wait(q, deps)
        if key not in self.dsem:
            self.dsem[key] = [self.es.enter_context(self.nc.semaphore("d_" + key)), 0]
        s = self.dsem[key]
        ins = self.eng[q].dma_start(out=out, in_=in_, **kw)
        s[1] += 16
        ins.then_inc(s[0], 16)
        return (s[0], s[1], "d_" + key)


class LinearT:
    def __init__(self, cx, name, KT, mwmax=128, nstage=3):
        self.cx = cx
        self.KT = KT
        self.nstage = nstage
        self.wst = cx.sb(name + "_wst", [128, nstage, KT, mwmax], F32)
        self.wbf = cx.sb(name + "_wbf", [128, 2, KT, mwmax], BF16)
        self.conv_tok = {}
        self.mm_tok = {}
        self.gen = 0
        self.name = name

    def run(self, act, act_deps, W, N, banks, bank_tok, epi, post=None, ntok=1024, conv_eng="pool", hook=None):
        cx = self.cx
        KT = self.KT
        Wv = W.rearrange("(kt p) n -> p kt n", p=128)
        n_tiles = (N + 127) // 128
        nh = ntok // 512
        pending_post = None
        for ft in range(n_tiles):
            g = self.gen
            self.gen += 1
            mw = min(128, N - ft * 128)
            s3 = g % self.nstage
            s2 = g % 2
            t_ld = cx.dma("sp", self.wst[:, s3, :, :mw], Wv[:, :, ft * 128: ft * 128 + mw],
                          key=f"{self.name}w{s3}", deps=[self.conv_tok.get(g - self.nstage)])
            self.conv_tok[g] = cx.op(
                conv_eng, lambda e: e.tensor_copy(out=self.wbf[:, s2, :, :mw], in_=self.wst[:, s3, :, :mw]),
                deps=[t_ld, self.mm_tok.get(g - 2)])
            last = None
            for h in range(nh):
                bi = (2 * g + h) % len(banks)
                bank = banks[bi]
                for kt in range(KT):
                    last = cx.op("pe", lambda e: e.matmul(
                        bank[:mw, :], lhsT=self.wbf[:, s2, kt, :mw], rhs=act[:, kt, h * 512:(h + 1) * 512],
                        start=(kt == 0), stop=(kt == KT - 1)),
                        deps=[self.conv_tok[g], bank_tok[bi], act_deps] if kt == 0 else [])
                bank_tok[bi] = epi(ft, h, bank[:mw, :], mw, last)
            self.mm_tok[g] = last
            if pending_post is not None:
                pending_post()
            pending_post = (lambda f=ft: post(f)) if post is not None else None
            if hook is not None:
                hook(ft)
        if pending_post is not None:
            pending_post()


def rstd_from_banks(cx, ssq_banks, rstd, last, ntok=1024):
    t = None
    for h in range(ntok // 512):
        t = cx.op("dve", lambda e: e.tensor_scalar(out=rstd[:, h * 512:(h + 1) * 512], in0=ssq_banks[h][:, :],
                                                   scalar1=RMS_EPS, scalar2=None, op0=ALU.add), deps=[last])
    t = cx.op("act", lambda e: e.activation(out=rstd[:, :ntok], in_=rstd[:, :ntok], func=AF.Sqrt), deps=[t])
    t = cx.op("dve", lambda e: e.reciprocal(out=rstd[:, :ntok], in_=rstd[:, :ntok]), deps=[t])
    return t


def rms_stats(cx, src_fn, n_kt, ones, sq, ssq_banks, rstd, src_deps, ntok=1024):
    nslot = sq.shape[1]
    sq_free = [None] * nslot
    last = None
    nh = ntok // 512
    for kt in range(n_kt):
        sl = kt % nslot
        t_sq = cx.op("act", lambda e: e.activation(out=sq[:, sl, :ntok], in_=src_fn(kt), func=AF.Square),
                     deps=[sq_free[sl], src_deps(kt)])
        for h in range(nh):
            last = cx.op("pe", lambda e: e.matmul(ssq_banks[h][:, :], lhsT=ones[:, :],
                                                  rhs=sq[:, sl, h * 512:(h + 1) * 512],
                                                  start=(kt == 0), stop=(kt == n_kt - 1)),
                         deps=[t_sq])
        sq_free[sl] = last
    t = rstd_from_banks(cx, ssq_banks, rstd, last, ntok)
    return t


GROUPS8 = [list(range(8))]
GROUPS4 = [[0, 1, 2, 3], [4, 5, 6, 7]]


def stage_norm_linear(cx, xT, jobs, bfl=None):
    with cx.scope():
        xs = cx.sb("xs", [128, 16, TOK], F32)
        xn = cx.sb("xn", [128, 16, TOK], BF16)
        sq = cx.sb("sq", [128, 4, TOK], BF16)
        rstd = cx.sb("rstd", [128, TOK], F32)
        ones = cx.sb("ones", [128, 128], BF16)
        ost = cx.sb("ost", [128, 4, 512], BF16)
        ssq = [cx.ps(f"ssq{h}", [128, 512]) for h in range(2)]
        banks = [cx.ps(f"acc{i}", [128, 512]) for i in range(4)]
        lin = LinearT(cx, "l", 16)
        t_ones = cx.op("pool", lambda e: e.memset(ones[:, :], 1.0 / D))
        if bfl is not None:
            fsb = cx.sb("fsb", [16, TOK], F32)
            flsb = cx.sb("flsb", [128, 8, 16], F32)
            onesF = cx.sb("onesF", [16, 16], F32)
            id16 = cx.sb("id16", [16, 16], F32)
            pF = cx.ps("pF", [128, 8, 16])
            cx.op("pool", lambda e: e.memset(onesF[:], 1.0))
            t_id16 = cx.op("pool", lambda e: e.affine_select(out=id16[:], in_=onesF[:], pattern=[[-1, 16]],
                                                             compare_op=ALU.is_equal, fill=0.0, base=0, channel_multiplier=1),
                           deps=[(cx.psem["pool"], cx.cnt["pool"], "p_pool")])
        xv = xT.rearrange("(kt p) t -> p kt t", p=128)
        t_x = [cx.dma("sp", xs[:, 4 * i:4 * i + 4, :], xv[:, 4 * i:4 * i + 4, :], key=f"x{i}") for i in range(4)]
        t_r = rms_stats(cx, lambda kt: xs[:, kt, :], 16, ones, sq, ssq, rstd, lambda kt: [t_x[kt // 4], t_ones])
        bank_tok = [None] * 4
        prev_mm = None
        tail_tok = []
        for (gc, gdeps, W, N, route, hook) in jobs:
            t_xn = None
            for kt in range(16):
                t_xn = cx.op("dve", lambda e: e.scalar_tensor_tensor(
                    out=xn[:, kt, :], in0=xs[:, kt, :], scalar=gc[:, kt:kt + 1], in1=rstd[:, :],
                    op0=ALU.mult, op1=ALU.mult), deps=[t_r, gdeps, prev_mm])
            store = make_store_epi(cx, ost, route)

            def epi(ft, h, ps, mw, tok):
                if route(ft) is not None:
                    return store(ft, h, ps, mw, tok)
                t_c = cx.op("act", lambda e: e.activation(out=fsb[0:mw, h * 512:(h + 1) * 512], in_=ps, func=AF.Copy), deps=[tok])
                tail_tok.append(t_c)
                return t_c
            lin.run(xn, [t_xn], W, N, banks, bank_tok, epi, hook=hook)
            prev_mm = lin.mm_tok[lin.gen - 1]
            if tail_tok and bfl is not None:
                t_t = None
                for kt in range(8):
                    t_t = cx.op("pe", lambda e: e.transpose(out=pF[:, kt, :], in_=fsb[0:16, kt * 128:(kt + 1) * 128],
                                                            identity=id16[:]), deps=[tail_tok, t_id16])
                t_c = cx.op("dve", lambda e: e.tensor_copy(out=flsb[:], in_=pF[:]), deps=[t_t])
                cx.dma("sp", bfl.rearrange("p (k h) -> p k h", h=16), flsb[:], key="fl", deps=[t_c])
                tail_tok = []
        cx.barrier()


def make_store_epi(cx, ost, route):
    nsl = ost.shape[1]
    free = [None] * nsl
    st = {"i": 0}

    def epi(ft, h, ps, mw, tok):
        dst, row0, func = route(ft)
        sl = st["i"] % nsl
        st["i"] += 1
        t_c = cx.op("act", lambda e: e.activation(out=ost[:mw, sl, :], in_=ps, func=func), deps=[tok, free[sl]])
        free[sl] = cx.dma("act", dst[row0:row0 + mw, h * 512:(h + 1) * 512], ost[:mw, sl, :], key=f"o{sl}", deps=[t_c])
        return t_c
    return epi


def stage_gate_out(cx, glu, a_src, idx_a, sz_d, resT, gc, bc, g_deps, w_out, w_glu, hT, unperm):
    zv = sz_d.rearrange("(kt p) t -> p kt t", p=128)
    rv = resT.rearrange("(kt p) t -> p kt t", p=128)
    hv = hT.rearrange("(kt p) t -> p kt t", p=128)
    with cx.scope():
        abf = cx.sb("abf", [128, 16, TOK], BF16)
        sz = cx.sb("sz", [128, 16, TOK], BF16)
        mT = cx.sb("mT", [128, 16, TOK], F32)
        sq = cx.sb("sq", [128, 4, 512], BF16)
        sig = cx.sb("sig", [128, 4, 512], BF16)
        ld = cx.sb("ld", [128, 4, TOK], F32)
        ho = cx.sb("ho", [128, 2, TOK], F32)
        rstd = cx.sb("rstd", [128, TOK], F32)
        ones = cx.sb("ones", [128, 128], BF16)
        ssq = [cx.ps(f"ssq{h}", [128, 512]) for h in range(2)]
        banks = [cx.ps(f"acc{i}", [128, 512]) for i in range(4)]
        lin = LinearT(cx, "l", 16, nstage=2)
        t_ones = cx.op("pool", lambda e: e.memset(ones[:, :], 1.0 / D))
        t_y2 = []
        for kt in range(16):
            ta = cx.idma(abf[:, kt, :], a_src, idx_a[:, kt:kt + 1], key=f"ia{kt % 4}", deps=[g_deps])
            tz = cx.dma("sp", sz[:, kt, :], zv[:, kt, :], key=f"lz{kt % 4}")
            if not glu:
                ty = cx.op("dve", lambda e: e.tensor_tensor(out=sz[:, kt, :], in0=abf[:, kt, :], in1=sz[:, kt, :],
                                                            op=ALU.mult), deps=[ta, tz])
            else:
                ty = [ta, tz]
            t_y2.append(ty)
        bank_tok = [None] * 4
        if glu:
            st = {"i": 0}
            sig_free = [None] * 4
            y2 = []

            def epi_glu(ft, h, ps, mw, tok):
                sl = st["i"] % 4
                st["i"] += 1
                t_s = cx.op("act", lambda e: e.activation(out=sig[:, sl, :], in_=ps, func=AF.Sigmoid,
                                                          bias=bc[:, ft:ft + 1]), deps=[tok, sig_free[sl], g_deps])
                hs = slice(h * 512, (h + 1) * 512)
                t_p = cx.op("dve", lambda e: e.tensor_tensor(out=sig[:, sl, :], in0=sig[:, sl, :], in1=abf[:, ft, hs],
                                                             op=ALU.mult), deps=[t_s])
                t_p = cx.op("dve", lambda e: e.tensor_tensor(out=sz[:, ft, hs], in0=sig[:, sl, :], in1=sz[:, ft, hs],
                                                             op=ALU.mult), deps=[t_p])
                sig_free[sl] = t_p
                y2.append(t_p)
                return t_s
            lin.run(abf, t_y2, w_glu, D, banks, bank_tok, epi_glu)
            y2_deps = y2
        else:
            y2_deps = t_y2
        st2 = {"i": 0}
        sq_free = [None] * 4
        sq_tok = {}

        def epi_out(ft, h, ps, mw, tok):
            sl = st2["i"] % 4
            st2["i"] += 1
            hs = slice(h * 512, (h + 1) * 512)
            cx.op("act", lambda e: e.activation(out=mT[:, ft, hs], in_=ps, func=AF.Copy), deps=[tok])
            t_q = cx.op("act", lambda e: e.activation(out=sq[:, sl, :], in_=ps, func=AF.Square), deps=[sq_free[sl]])
            sq_tok[(ft, h)] = (t_q, sl)
            return t_q
        fin = {"t": None}

        def post(ft):
            for h in range(2):
                t_q, sl = sq_tok[(ft, h)]
                t = cx.op("pe", lambda e: e.matmul(ssq[h][:, :], lhsT=ones[:, :], rhs=sq[:, sl, :],
                                                   start=(ft == 0), stop=(ft == 15)), deps=[t_q, t_ones])
                sq_free[sl] = t
                fin["t"] = t
        lin.run(sz, y2_deps, w_out, D, banks, bank_tok, epi_out, post=post)
        t_r = rstd_from_banks(cx, ssq, rstd, fin["t"])
        ld_free = [None] * 4
        ho_free = [None] * 2
        for ft in range(16):
            sr, so, sh = (2 * ft) % 4, (2 * ft + 1) % 4, ft % 2
            t1 = cx.dma("sp", ld[:, sr, :], rv[:, ft, :], key=f"ld{sr}", deps=[ld_free[sr]])
            tm = cx.op("dve", lambda e: e.scalar_tensor_tensor(out=ld[:, so, :], in0=mT[:, ft, :], scalar=gc[:, ft:ft + 1],
                                                               in1=rstd[:, :], op0=ALU.mult, op1=ALU.mult),
                       deps=[t_r, g_deps, ld_free[so]])
            if unperm:
                o_ap = ho[:, sh, :].rearrange("p (j s) -> p s j", s=8)
                i0 = ld[:, so, :].rearrange("p (s j) -> p s j", s=8)
                i1 = ld[:, sr, :].rearrange("p (s j) -> p s j", s=8)
            else:
                o_ap, i0, i1 = ho[:, sh, :], ld[:, so, :], ld[:, sr, :]
            ta = cx.op("dve", lambda e: e.tensor_tensor(out=o_ap, in0=i0, in1=i1, op=ALU.add), deps=[tm, t1, ho_free[sh]])
            ho_free[sh] = cx.dma("act", hv[:, ft, :], ho[:, sh, :], key=f"ho{sh}", deps=[ta])
            ld_free[sr] = ta
            ld_free[so] = ta
        cx.barrier()


def stage_s5(cx, gu, idx1, idx_deps, pars, Bri, Cri, dcol, by):
    GPC, NCH, NB = 16, 1024, 2
    NJ = NCH // NB
    NBLK = NCH // 512
    G = GPC
    with cx.scope():
        par = cx.sb("par", [64, 3, G], F32)
        B_ = cx.sb("B_", [64, 2, G, 16], F32)
        C_ = cx.sb("C_", [64, 2, G, 16], F32)
        dc = cx.sb("dc", [128, G], F32)
        sc = cx.sb("sc", [64, 24, G], F32)
        Pn = cx.sb("Pn", [64, 2, 8, G], F32)
        Pp = cx.sb("Pp", [64, 2, 16, G], F32)
        Bc = cx.sb("Bc", [64, 2, G, 16], F32)
        tb = cx.sb("tb", [64, 4, G, 16], F32)
        Bs = cx.sb("Bs", [64, 2, G, 128], BF16)
        Ct = cx.sb("Ct", [64, 2, G, 256], BF16)
        TT = cx.sb("TT", [128, G, 128], BF16)
        GT = cx.sb("GT", [128, G, 128], BF16)
        maskT = cx.sb("maskT", [128, 128], F32)
        identF = cx.sb("identF", [128, 128], F32)
        identB = cx.sb("identB", [128, 128], BF16)
        onesF = cx.sb("onesF", [128, 128], F32)
        onesB = cx.sb("onesB", [128, 16], BF16)
        Sel = cx.sb("Sel", [128, 64, 128], BF16)
        tmpT = cx.sb("tmpT", [128, 2, 128], F32)
        cst = cx.sb("cst", [64, 2], F32)
        Ubf = cx.sb("Ubf", [128, G, NCH], BF16)
        utyg = cx.sb("utyg", [128, 8, 1024], BF16)
        Ygb = cx.sb("Ygb", [128, 8, NCH], BF16)
        WS = cx.sb("WS", [64, 2, NB, G, NJ + 1], BF16)
        NS = 2 * NB * G
        S = cx.sb("S", [64, NS], F32)
        A8R = cx.sb("A8R", [64, NS], F32)
        A8I = cx.sb("A8I", [64, NS], F32)
        t1 = cx.sb("t1", [64, NS], F32)
        t2 = cx.sb("t2", [64, NS], F32)
        PM = cx.sb("PM", [64, 2, 7, G], F32)
        pT = cx.ps("pT", [128, 128])
        pG = cx.ps("pG", [128, 128])
        pW = [[cx.ps(f"pW{i}{h}", [128, 512]) for h in range(2)] for i in range(2)]
        pY = [cx.ps(f"pY{i}", [128, 512]) for i in range(2)]

        t_par = cx.dma("sp", par[:], pars[:, :, :], key="c0")
        t_B = cx.dma("sp", B_[:], Bri[:, :, :, :], key="c1")
        t_C = cx.dma("sp", C_[:], Cri[:, :, :, :], key="c2")
        t_d = cx.dma("sp", dc[:], dcol[:, :], key="c3")
        tp = cx.op("pool", lambda e: e.memset(onesF[:], 1.0))
        t_mask = cx.op("pool", lambda e: e.affine_select(
            out=maskT[:].rearrange("p (t c) -> p t c", c=16), in_=onesF[:].rearrange("p (t c) -> p t c", c=16),
            pattern=[[16, 8], [0, 16]], compare_op=ALU.is_ge, fill=0.0, base=15, channel_multiplier=-1), deps=[tp])
        t_idF = cx.op("pool", lambda e: e.affine_select(
            out=identF[:], in_=onesF[:], pattern=[[-1, 128]], compare_op=ALU.is_equal, fill=0.0, base=0,
            channel_multiplier=1), deps=[tp])
        t_idB = cx.op("pool", lambda e: e.tensor_copy(out=identB[:], in_=identF[:]), deps=[t_idF])
        cx.op("pool", lambda e: e.memset(cst[:, 0:1], float(np.pi / 2)))
        t_cst = cx.op("pool", lambda e: e.memset(cst[:, 1:2], 0.0))
        t_ws0 = cx.op("pool", lambda e: e.memset(WS[:, :, :, :, 0:1], 0.0))
        t_s0 = cx.op("pool", lambda e: e.memset(S[:], 0.0))
        t_o16 = cx.op("pool", lambda e: e.memset(onesB[:], 1.0))
        t_sel = cx.op("pool", lambda e: e.memset(Sel[:], 0.0))
        for a in range(8):
            for b in range(8):
                t_sel = cx.op("pool", lambda e: e.affine_select(
                    out=Sel[:, a * 8 + b, b * 16:(b + 1) * 16], in_=onesB[:], pattern=[[-1, 16]], compare_op=ALU.is_equal,
                    fill=0.0, base=-a * 16, channel_multiplier=1), deps=[t_sel, t_o16])

        t_u = [None] * G
        ut_free = None
        k = 0
        pY_free = [None, None]
        for ftl in range(2):
            t_g = [cx.idma(utyg[:, r, :], gu, idx1[:, r * 2 + ftl:r * 2 + ftl + 1], key=f"iu{r % 4}", deps=[idx_deps, ut_free])
                   for r in range(8)]
            last_mm = None
            for gl in range(8):
                g = ftl * 8 + gl
                for bb in range(2):
                    i = k % 2
                    k += 1
                    for s in range(8):
                        last_mm = cx.op("pe", lambda e: e.matmul(
                            pY[i][:, :], lhsT=Sel[:, gl * 8 + s, :], rhs=utyg[:, 4 * bb:4 * bb + 4, s * 128:(s + 1) * 128],
                            start=(s == 0), stop=(s == 7)), deps=[t_sel, t_g, pY_free[i]] if s == 0 else [])
                    t_e = cx.op("act" if k % 2 else "dve",
                                (lambda e: e.activation(out=Ubf[:, g, bb * 512:(bb + 1) * 512], in_=pY[i][:, :], func=AF.Copy))
                                if k % 2 else
                                (lambda e: e.tensor_copy(out=Ubf[:, g, bb * 512:(bb + 1) * 512], in_=pY[i][:, :])),
                                deps=[last_mm])
                    pY_free[i] = t_e
                    t_u[g] = [t_u[g], t_e]
            ut_free = last_mm

        last = {"t": [t_par, t_cst]}

        def V(fn, extra=()):
            last["t"] = cx.op("dve", fn, deps=[last["t"]] + list(extra))
            return last["t"]

        def A(fn, extra=()):
            last["t"] = cx.op("act", fn, deps=[last["t"]] + list(extra))
            return last["t"]

        def s_(i):
            return sc[:, i, :]
        ar, ai, ldt = par[:, 0, :], par[:, 1, :], par[:, 2, :]
        DT, X1, MAG, TH, SN, CS = range(6)
        A(lambda e: e.activation(out=s_(DT), in_=ldt, func=AF.Exp))
        V(lambda e: e.tensor_tensor(out=s_(X1), in0=ar, in1=s_(DT), op=ALU.mult))
        A(lambda e: e.activation(out=s_(MAG), in_=s_(X1), func=AF.Exp, scale=1.0 / 16))
        V(lambda e: e.tensor_tensor(out=s_(TH), in0=ai, in1=s_(DT), op=ALU.mult))
        A(lambda e: e.activation(out=s_(SN), in_=s_(TH), func=AF.Sin, scale=1.0 / 16))
        A(lambda e: e.activation(out=s_(CS), in_=s_(TH), func=AF.Sin, scale=-1.0 / 16, bias=cst[:, 0:1]))
        R0, I0, R1, I1, TA, TB = 6, 7, 8, 9, 10, 11
        V(lambda e: e.tensor_tensor(out=s_(R0), in0=s_(MAG), in1=s_(CS), op=ALU.mult))
        V(lambda e: e.tensor_tensor(out=s_(I0), in0=s_(MAG), in1=s_(SN), op=ALU.mult))

        def cmul(oR, oI, aR, aI, bR, bI):
            V(lambda e: e.tensor_tensor(out=s_(TA), in0=aR, in1=bR, op=ALU.mult))
            V(lambda e: e.tensor_tensor(out=s_(TB), in0=aI, in1=bI, op=ALU.mult))
            V(lambda e: e.tensor_tensor(out=oR, in0=s_(TA), in1=s_(TB), op=ALU.subtract))
            V(lambda e: e.tensor_tensor(out=s_(TA), in0=aR, in1=bI, op=ALU.mult))
            V(lambda e: e.tensor_tensor(out=s_(TB), in0=aI, in1=bR, op=ALU.mult))
            V(lambda e: e.tensor_tensor(out=oI, in0=s_(TA), in1=s_(TB), op=ALU.add))
        cur, nxt = (R0, I0), (R1, I1)
        for _ in range(4):
            cmul(s_(nxt[0]), s_(nxt[1]), s_(cur[0]), s_(cur[1]), s_(cur[0]), s_(cur[1]))
            cur, nxt = nxt, cur
        AR, AI = cur
        DEN, NR, CR, CI, M2 = 12, 13, 14, 15, 16
        V(lambda e: e.tensor_tensor(out=s_(TA), in0=ar, in1=ar, op=ALU.mult))
        V(lambda e: e.tensor_tensor(out=s_(TB), in0=ai, in1=ai, op=ALU.mult))
        V(lambda e: e.tensor_tensor(out=s_(DEN), in0=s_(TA), in1=s_(TB), op=ALU.add))
        V(lambda e: e.reciprocal(out=s_(DEN), in_=s_(DEN)))
        V(lambda e: e.tensor_scalar(out=s_(NR), in0=s_(AR), scalar1=-1.0, scalar2=None, op0=ALU.add))
        V(lambda e: e.tensor_tensor(out=s_(TA), in0=s_(NR), in1=ar, op=ALU.mult))
        V(lambda e: e.tensor_tensor(out=s_(TB), in0=s_(AI), in1=ai, op=ALU.mult))
        V(lambda e: e.tensor_tensor(out=s_(TA), in0=s_(TA), in1=s_(TB), op=ALU.add))
        V(lambda e: e.tensor_tensor(out=s_(CR), in0=s_(TA), in1=s_(DEN), op=ALU.mult))
        V(lambda e: e.tensor_tensor(out=s_(TA), in0=s_(AI), in1=ar, op=ALU.mult))
        V(lambda e: e.tensor_tensor(out=s_(TB), in0=s_(NR), in1=ai, op=ALU.mult))
        V(lambda e: e.tensor_tensor(out=s_(TA), in0=s_(TA), in1=s_(TB), op=ALU.subtract))
        V(lambda e: e.tensor_tensor(out=s_(CI), in0=s_(TA), in1=s_(DEN), op=ALU.mult))
        V(lambda e: e.tensor_copy(out=Pp[:, 0, 0, :], in_=onesF[0:64, 0:G]), [tp])
        V(lambda e: e.tensor_copy(out=Pn[:, 0, 0, :], in_=onesF[0:64, 0:G]))
        V(lambda e: e.memset(Pp[:, 1, 0, :], 0.0))
        V(lambda e: e.memset(Pn[:, 1, 0, :], 0.0))
        V(lambda e: e.tensor_copy(out=Pp[:, 0, 1, :], in_=s_(AR)))
        V(lambda e: e.tensor_copy(out=Pp[:, 1, 1, :], in_=s_(AI)))
        V(lambda e: e.tensor_tensor(out=s_(TA), in0=s_(AR), in1=s_(AR), op=ALU.mult))
        V(lambda e: e.tensor_tensor(out=s_(TB), in0=s_(AI), in1=s_(AI), op=ALU.mult))
        V(lambda e: e.tensor_tensor(out=s_(M2), in0=s_(TA), in1=s_(TB), op=ALU.add))
        V(lambda e: e.reciprocal(out=s_(M2), in_=s_(M2)))
        V(lambda e: e.tensor_tensor(out=Pn[:, 0, 1, :], in0=s_(AR), in1=s_(M2), op=ALU.mult))
        V(lambda e: e.scalar_tensor_tensor(out=Pn[:, 1, 1, :], in0=s_(AI), scalar=-1.0, in1=s_(M2),
                                           op0=ALU.mult, op1=ALU.mult))
        for kk in range(2, 16):
            cmul(Pp[:, 0, kk, :], Pp[:, 1, kk, :], Pp[:, 0, kk - 1, :], Pp[:, 1, kk - 1, :], Pp[:, 0, 1, :], Pp[:, 1, 1, :])
        for kk in range(2, 8):
            cmul(Pn[:, 0, kk, :], Pn[:, 1, kk, :], Pn[:, 0, kk - 1, :], Pn[:, 1, kk - 1, :], Pn[:, 0, 1, :], Pn[:, 1, 1, :])
        V(lambda e: e.tensor_copy(out=PM[:, 0, 0, :], in_=Pp[:, 0, 8, :]))
        V(lambda e: e.tensor_copy(out=PM[:, 1, 0, :], in_=Pp[:, 1, 8, :]))
        for i in range(1, 7):
            cmul(PM[:, 0, i, :], PM[:, 1, i, :], PM[:, 0, i - 1, :], PM[:, 1, i - 1, :], PM[:, 0, i - 1, :], PM[:, 1, i - 1, :])
        for r in range(2 * NB):
            V(lambda e: e.tensor_copy(out=A8R[:, r * G:(r + 1) * G], in_=PM[:, 0, 6, :]))
        for r in range(NB):
            V(lambda e: e.tensor_scalar(out=A8I[:, r * G:(r + 1) * G], in0=PM[:, 1, 6, :], scalar1=-1.0, scalar2=None,
                                        op0=ALU.mult))
            V(lambda e: e.tensor_copy(out=A8I[:, (NB + r) * G:(NB + r + 1) * G], in_=PM[:, 1, 6, :]))
        t_pm = last["t"]

        def bc(ap2d):
            return ap2d.unsqueeze(2).to_broadcast([64, G, 16])
        V(lambda e: e.tensor_tensor(out=tb[:, 0], in0=B_[:, 0], in1=bc(s_(CR)), op=ALU.mult), [t_B])
        V(lambda e: e.tensor_tensor(out=tb[:, 1], in0=B_[:, 1], in1=bc(s_(CI)), op=ALU.mult))
        V(lambda e: e.tensor_tensor(out=Bc[:, 0], in0=tb[:, 0], in1=tb[:, 1], op=ALU.subtract))
        V(lambda e: e.tensor_tensor(out=tb[:, 0], in0=B_[:, 1], in1=bc(s_(CR)), op=ALU.mult))
        V(lambda e: e.tensor_tensor(out=tb[:, 1], in0=B_[:, 0], in1=bc(s_(CI)), op=ALU.mult))
        V(lambda e: e.tensor_tensor(out=Bc[:, 1], in0=tb[:, 0], in1=tb[:, 1], op=ALU.add))
        BsV = [Bs[:, h].rearrange("p g (s c) -> p g s c", c=16) for h in range(2)]
        CtV = [Ct[:, h].rearrange("p g (t c) -> p g t c", c=16) for h in range(2)]
        for s in range(8):
            pr, pi = bc(Pn[:, 0, s, :]), bc(Pn[:, 1, s, :])
            V(lambda e: e.tensor_tensor(out=tb[:, 0], in0=Bc[:, 0], in1=pr, op=ALU.mult))
            V(lambda e: e.tensor_tensor(out=tb[:, 1], in0=Bc[:, 1], in1=pi, op=ALU.mult))
            V(lambda e: e.tensor_tensor(out=BsV[0][:, :, s, :], in0=tb[:, 0], in1=tb[:, 1], op=ALU.subtract))
            V(lambda e: e.tensor_tensor(out=tb[:, 0], in0=Bc[:, 0], in1=pi, op=ALU.mult))
            V(lambda e: e.tensor_tensor(out=tb[:, 1], in0=Bc[:, 1], in1=pr, op=ALU.mult))
            V(lambda e: e.tensor_tensor(out=BsV[1][:, :, s, :], in0=tb[:, 0], in1=tb[:, 1], op=ALU.add))
        t_bs = last["t"]
        for t in range(16):
            pr, pi = bc(Pp[:, 0, t, :]), bc(Pp[:, 1, t, :])
            V(lambda e: e.tensor_tensor(out=tb[:, 0], in0=C_[:, 0], in1=pr, op=ALU.mult), [t_C])
            V(lambda e: e.tensor_tensor(out=tb[:, 1], in0=C_[:, 1], in1=pi, op=ALU.mult))
            V(lambda e: e.tensor_tensor(out=CtV[0][:, :, t, :], in0=tb[:, 0], in1=tb[:, 1], op=ALU.subtract))
            V(lambda e: e.tensor_tensor(out=tb[:, 0], in0=C_[:, 0], in1=pi, op=ALU.mult))
            V(lambda e: e.tensor_tensor(out=tb[:, 1], in0=C_[:, 1], in1=pr, op=ALU.mult))
            V(lambda e: e.scalar_tensor_tensor(out=CtV[1][:, :, t, :], in0=tb[:, 0], scalar=-1.0, in1=tb[:, 1],
                                               op0=ALU.mult, op1=ALU.subtract))
        t_ct = last["t"]

        t_TT, t_GT = [], []
        pT_free = None
        pG_free = None
        for gl in range(G):
            cx.op("pe", lambda e: e.matmul(pT[:, :], lhsT=Bs[:, 0, gl, :], rhs=Ct[:, 0, gl, 0:128], start=True, stop=False),
                  deps=[t_bs, t_ct, pT_free])
            t_m = cx.op("pe", lambda e: e.matmul(pT[:, :], lhsT=Bs[:, 1, gl, :], rhs=Ct[:, 1, gl, 0:128], start=False, stop=True))
            sl = gl % 2
            t_a = cx.op("dve", lambda e: e.tensor_tensor(out=tmpT[:, sl, :], in0=pT[:, :], in1=maskT[:], op=ALU.mult),
                        deps=[t_m, t_mask, last["t"]])
            pT_free = t_a
            t_b = cx.op("dve", lambda e: e.scalar_tensor_tensor(out=TT[:, gl, :], in0=identF[:], scalar=dc[:, gl:gl + 1],
                                                                in1=tmpT[:, sl, :], op0=ALU.mult, op1=ALU.add),
                        deps=[t_a, t_idF, t_d])
            last["t"] = t_b
            t_TT.append(t_b)
            cx.op("pe", lambda e: e.matmul(pG[:, 0:64], lhsT=Bs[:, 0, gl, :], rhs=identB[0:64, 0:64], start=True, stop=True),
                  deps=[t_idB, pG_free])
            t_m2 = cx.op("pe", lambda e: e.matmul(pG[:, 64:128], lhsT=Bs[:, 1, gl, :], rhs=identB[0:64, 0:64], start=True, stop=True))
            t_g = cx.op("act", lambda e: e.activation(out=GT[:, gl, :], in_=pG[:, :], func=AF.Copy), deps=[t_m2])
            pG_free = t_g
            t_GT.append(t_g)

        pW_free = [[None, None], [None, None]]
        t_W = None
        k = 0
        for gl in range(G):
            for blk in range(NBLK):
                i = k % 2
                k += 1
                b, j0 = (blk * 512) // NJ, (blk * 512) % NJ
                for h in range(2):
                    t_m = cx.op("pe", lambda e: e.matmul(pW[i][h][0:64, :], lhsT=GT[:, gl, h * 64:(h + 1) * 64],
                                                         rhs=Ubf[:, gl, blk * 512:(blk + 1) * 512], start=True, stop=True),
                                deps=[t_GT[gl], t_u[gl], pW_free[i][h]])
                    dst = WS[:, h, b, gl, 1 + j0:1 + j0 + 512]
                    t_e = cx.op("act", lambda e: e.activation(out=dst, in_=pW[i][h][0:64, :], func=AF.Copy), deps=[t_m])
                    pW_free[i][h] = t_e
                    t_W = t_e

        NBK, BL = 8, 64
        R1 = Ygb[0:64, :, :].rearrange("p a b -> p (a b)").bitcast(F32)
        R2 = utyg[0:64, :, :].rearrange("p a b -> p (a b)").bitcast(F32)
        S2, u1, u2, A8Rx, A8Ix, Cc = [R1[:, i * 512:(i + 1) * 512] for i in range(6)]
        PW = R2[:, 0:2048]
        ta, tb2 = R2[:, 2048:3072], R2[:, 3072:4096]
        v5 = lambda ap: ap.rearrange("p (h b g k) -> p h b g k", h=2, b=NB, g=G)
        t = cx.op("dve", lambda e: e.memset(S2, 0.0), deps=[t_W, last["t"], t_pm])
        t = cx.op("dve", lambda e: e.tensor_copy(
            out=A8Rx.rearrange("p (x g k) -> p x g k", g=G, k=NBK),
            in_=PM[:, 0, 0, :].unsqueeze(1).unsqueeze(3).to_broadcast([64, 2 * NB, G, NBK])), deps=[t])
        a8i_b = PM[:, 1, 0, :].unsqueeze(1).unsqueeze(3).to_broadcast([64, NB, G, NBK])
        t = cx.op("dve", lambda e: e.tensor_scalar(out=v5(A8Ix)[:, 0], in0=a8i_b, scalar1=-1.0, scalar2=None, op0=ALU.mult), deps=[t])
        t = cx.op("dve", lambda e: e.tensor_copy(out=v5(A8Ix)[:, 1], in_=a8i_b), deps=[t])
        PW4 = PW.rearrange("p (h g j) -> p h g j", h=2, g=G)
        t = cx.op("dve", lambda e: e.tensor_copy(out=PW4[:, 0, :, 0], in_=PM[:, 0, 0, :]), deps=[t, ut_free])
        t = cx.op("dve", lambda e: e.tensor_copy(out=PW4[:, 1, :, 0], in_=PM[:, 1, 0, :]), deps=[t])
        for i in range(6):
            m = 1 << i
            mr = PM[:, 0, i, :].unsqueeze(2).to_broadcast([64, G, m])
            mi = PM[:, 1, i, :].unsqueeze(2).to_broadcast([64, G, m])
            xa = ta[:, 0:G * m].rearrange("p (g j) -> p g j", g=G)
            xb = tb2[:, 0:G * m].rearrange("p (g j) -> p g j", g=G)
            pr, pi = PW4[:, 0, :, 0:m], PW4[:, 1, :, 0:m]
            t = cx.op("dve", lambda e: e.tensor_tensor(out=xa, in0=pr, in1=mr, op=ALU.mult), deps=[t])
            t = cx.op("dve", lambda e: e.tensor_tensor(out=xb, in0=pi, in1=mi, op=ALU.mult), deps=[t])
            t = cx.op("dve", lambda e: e.tensor_tensor(out=PW4[:, 0, :, m:2 * m], in0=xa, in1=xb, op=ALU.subtract), deps=[t])
            t = cx.op("dve", lambda e: e.tensor_tensor(out=xa, in0=pr, in1=mi, op=ALU.mult), deps=[t])
            t = cx.op("dve", lambda e: e.tensor_tensor(out=xb, in0=pi, in1=mr, op=ALU.mult), deps=[t])
            t = cx.op("dve", lambda e: e.tensor_tensor(out=PW4[:, 1, :, m:2 * m], in0=xa, in1=xb, op=ALU.add), deps=[t])
        S2h = S2.rearrange("p (h x) -> p h x", h=2)
        t_prev = [t, t_s0, t_ws0]
        st_tok = None
        for jl in range(BL):
            wj = WS[:, :, :, :, 1 + jl:2 + jl + (NBK - 1) * BL:BL]
            cx.op("dve", lambda e: e.tensor_tensor(out=u1, in0=S2, in1=A8Rx, op=ALU.mult), deps=[t_prev])
            tb_ = cx.op("dve", lambda e: e.tensor_tensor(out=u2.rearrange("p (h x) -> p h x", h=2), in0=S2h[:, ::-1, :],
                                                         in1=A8Ix.rearrange("p (h x) -> p h x", h=2), op=ALU.mult))
            tc = cx.op("dve", lambda e: e.tensor_tensor(out=v5(u2), in0=v5(u2), in1=wj, op=ALU.add), deps=[tb_])
            td = cx.op("dve", lambda e: e.tensor_tensor(out=S2, in0=u1, in1=u2, op=ALU.add), deps=[tc, st_tok])
            st_tok = cx.op("act", lambda e: e.activation(out=wj, in_=v5(S2), func=AF.Copy), deps=[td])
            t_prev = [td]
        C3 = Cc.rearrange("p (x k) -> p x k", k=NBK)
        C4 = Cc.rearrange("p (h x k) -> p h x k", h=2, k=NBK)
        S3 = S2.rearrange("p (x k) -> p x k", k=NBK)
        t = cx.op("dve", lambda e: e.memset(Cc, 0.0), deps=[td])
        for k in range(NBK - 1):
            t = cx.op("dve", lambda e: e.tensor_tensor(out=t1[:], in0=C3[:, :, k], in1=A8R[:], op=ALU.mult), deps=[t])
            t = cx.op("dve", lambda e: e.tensor_tensor(out=t2[:].rearrange("p (h x) -> p h x", h=2), in0=C4[:, ::-1, :, k],
                                                       in1=A8I[:].rearrange("p (h x) -> p h x", h=2), op=ALU.mult), deps=[t])
            t = cx.op("dve", lambda e: e.tensor_tensor(out=t2[:], in0=t2[:], in1=S3[:, :, k], op=ALU.add), deps=[t])
            t = cx.op("dve", lambda e: e.tensor_tensor(out=C3[:, :, k + 1], in0=t1[:], in1=t2[:], op=ALU.add), deps=[t])
        C5 = v5(Cc)
        xa = ta.rearrange("p (g j) -> p g j", g=G)
        xb = tb2.rearrange("p (g j) -> p g j", g=G)
        t = [t, st_tok]
        for b in range(NB):
            for k in range(1, NBK):
                cr = C5[:, 0, b, :, k].unsqueeze(2).to_broadcast([64, G, BL])
                ci = C5[:, 1, b, :, k].unsqueeze(2).to_broadcast([64, G, BL])
                wre = WS[:, 0, b, :, 1 + k * BL:1 + (k + 1) * BL]
                wim = WS[:, 1, b, :, 1 + k * BL:1 + (k + 1) * BL]
                t = cx.op("dve", lambda e: e.tensor_tensor(out=xa, in0=PW4[:, 0], in1=cr, op=ALU.mult), deps=[t])
                t = cx.op("dve", lambda e: e.tensor_tensor(out=xb, in0=PW4[:, 1], in1=ci, op=ALU.mult), deps=[t])
                t = cx.op("dve", lambda e: e.tensor_tensor(out=xa, in0=xa, in1=xb, op=ALU.subtract), deps=[t])
                t = cx.op("dve", lambda e: e.tensor_tensor(out=wre, in0=wre, in1=xa, op=ALU.add), deps=[t])
                t = cx.op("dve", lambda e: e.tensor_tensor(out=xa, in0=PW4[:, 0], in1=ci, op=ALU.mult), deps=[t])
                t = cx.op("dve", lambda e: e.tensor_tensor(out=xb, in0=PW4[:, 1], in1=cr, op=ALU.mult), deps=[t])
                t = cx.op("dve", lambda e: e.tensor_tensor(out=xa, in0=xa, in1=xb, op=ALU.add), deps=[t])
                t = cx.op("dve", lambda e: e.tensor_tensor(out=wim, in0=wim, in1=xa, op=ALU.add), deps=[t])
        st_tok = t

        pY_free = [pY_free[0], pY_free[1]]
        pR = [pW[0][0], pW[0][1], pW[1][0], pW[1][1]]
        pR_free = [pW_free[0][0], pW_free[0][1], pW_free[1][0], pW_free[1][1]]
        yg_dmas = []
        Ygb_free = None
        utyg_free = [ut_free]
        k = 0
        kr = 0
        for ftl in range(2):
            t_yg = []
            for gl8 in range(8):
                gl = ftl * 8 + gl8
                for blk in range(NBLK):
                    i = k % 2
                    k += 1
                    b, j0 = (blk * 512) // NJ, (blk * 512) % NJ
                    cx.op("pe", lambda e: e.matmul(pY[i][:, :], lhsT=TT[:, gl, :], rhs=Ubf[:, gl, blk * 512:(blk + 1) * 512],
                                                   start=True, stop=False), deps=[t_TT[gl], pY_free[i], st_tok])
                    cx.op("pe", lambda e: e.matmul(pY[i][:, :], lhsT=Ct[:, 0, gl, 128:256], rhs=WS[:, 0, b, gl, j0:j0 + 512],
                                                   start=False, stop=False))
                    t_m = cx.op("pe", lambda e: e.matmul(pY[i][:, :], lhsT=Ct[:, 1, gl, 128:256], rhs=WS[:, 1, b, gl, j0:j0 + 512],
                                                         start=False, stop=True))
                    t_e = cx.op("act", lambda e: e.activation(out=Ygb[:, gl8, blk * 512:(blk + 1) * 512], in_=pY[i][:, :],
                                                              func=AF.Gelu_apprx_tanh), deps=[t_m, Ygb_free])
                    pY_free[i] = t_e
                    t_yg.append(t_e)
            last_mm = None
            evs = []
            for t in range(8):
                for bb in range(2):
                    i = kr % 4
                    kr += 1
                    for gl8 in range(8):
                        last_mm = cx.op("pe", lambda e: e.matmul(pR[i][:, :], lhsT=Sel[:, t * 8 + gl8, :],
                                                                 rhs=Ygb[:, gl8, bb * 512:(bb + 1) * 512],
                                                                 start=(gl8 == 0), stop=(gl8 == 7)),
                                        deps=[t_yg, pR_free[i]] if gl8 == 0 else [])
                    dst = utyg[:, 4 * bb:4 * bb + 4, t * 128:(t + 1) * 128]
                    src = pR[i][:, :].rearrange("p (r j) -> p r j", r=4)
                    if kr % 2:
                        t_e = cx.op("act", lambda e: e.activation(out=dst, in_=src, func=AF.Copy), deps=[last_mm, utyg_free])
                    else:
                        t_e = cx.op("dve", lambda e: e.tensor_copy(out=dst, in_=src), deps=[last_mm, utyg_free])
                    pR_free[i] = t_e
                    evs.append(t_e)
            Ygb_free = last_mm
            t_d = cx.dma("sp", by[ftl * 128:(ftl + 1) * 128, :], utyg[:].rearrange("p r x -> p (r x)"), key="yst", deps=evs)
            utyg_free = [t_d]
            yg_dmas.append(t_d)
        cx.barrier()


DH = 128
ATT_SCALE = DH ** -0.5


def stage_attn(cx, gK, gV, gQ, idx3, gfl, idxf, sel16, BFd, in_deps, bo):
    NH, S = 4, 4096
    NQC = S // 512
    NKT = S // 128
    with cx.scope():
        flr = cx.sb("flr", [128, 4, 8, 16], F32)
        flt = cx.sb("flt", [128, NKT, 16], F32)
        s16 = cx.sb("s16", [128, NH, 16], F32)
        bfs = cx.sb("bfs", [128, NH], F32)
        Lc = cx.sb("Lc", [128, NH, NKT], F32)
        CL = cx.sb("CL", [128, NH, NKT], F32)
        Tot = cx.sb("Tot", [128, NH, NKT], F32)
        Exc = cx.sb("Exc", [128, NH, NKT], F32)
        Fac = cx.sb("Fac", [128, NH, NKT], F32)
        Bias = cx.sb("Bias", [128, NH, NKT, NKT], F32)
        onesF = cx.sb("onesF", [128, 128], F32)
        triF = cx.sb("triF", [128, 128], F32)
        triB = cx.sb("triB", [128, 128], BF16)
        onesB = cx.sb("onesB", [128, 128], BF16)
        identB = cx.sb("identB", [128, 128], BF16)
        one1 = cx.sb("one1", [128, 1], F32)
        vT = cx.sb("vT", [128, S], BF16)
        Qb = cx.sb("Qb", [128, 2, S], BF16)
        Kb = cx.sb("Kb", [128, 2, S], BF16)
        Vb = cx.sb("Vb", [128, 2, NKT, DH], BF16)
        NPS, NPT, LOOK = 3, 6, 2
        Pt = cx.sb("Pt", [128, NPT, 512], BF16)
        Od = cx.sb("Od", [128, 2, 512], F32)
        ofin = cx.sb("ofin", [128, 2, 512], F32)
        obf = cx.sb("obf", [128, 2, 512], BF16)
        dfin = cx.sb("dfin", [128, 2, 512], F32)
        dacc = {"o": cx.sb("dacc_o", [128, 512], F32), "d": cx.sb("dacc_d", [128, 512], F32)}
        pS = [cx.ps(f"pS{i}", [128, 512]) for i in range(NPS)]
        pOo = cx.ps("pOo", [128, 512])
        pOd = cx.ps("pOd", [128, 512])
        pDen = cx.ps("pDen", [128, 512])
        pDo = pDd = pDen
        pC = cx.ps("pC", [128, 2, NH * NKT])
        pV = cx.ps("pV", [128, 4, 128], BF16)
        dacc_free = {"o": None, "d": None}

        tp = cx.op("pool", lambda e: e.memset(onesF[:], 1.0))
        t_tri = cx.op("pool", lambda e: e.affine_select(out=triF[:], in_=onesF[:], pattern=[[1, 128]], compare_op=ALU.is_ge,
                                                        fill=0.0, base=0, channel_multiplier=-1), deps=[tp])
        t_triB = cx.op("pool", lambda e: e.tensor_copy(out=triB[:], in_=triF[:]), deps=[t_tri])
        t_oB = cx.op("pool", lambda e: e.tensor_copy(out=onesB[:], in_=onesF[:]), deps=[tp])
        t_o1 = cx.op("pool", lambda e: e.memset(one1[:], 1.0))
        t_id = cx.op("pool", lambda e: e.affine_select(out=identB[:], in_=onesB[:], pattern=[[-1, 128]], compare_op=ALU.is_equal,
                                                       fill=0.0, base=0, channel_multiplier=1), deps=[t_oB])
        t_fl = [cx.idma(flr[:, a].rearrange("p k h -> p (k h)"), gfl, idxf[:, a:a + 1], key=f"if{a}", deps=[in_deps])
                for a in range(4)]
        t_s16 = cx.dma("sp", s16[:], sel16[:, :, :], key="c1")
        t_bf = cx.dma("sp", bfs[:], BFd[:, :], key="c2")
        t = None
        flv = flr[:].rearrange("p a k h -> p (a k) h")
        for h in range(NH):
            t = cx.op("dve", lambda e: e.tensor_tensor(out=flt[:], in0=flv, in1=s16[:, h:h + 1, :].to_broadcast([128, NKT, 16]),
                                                       op=ALU.mult), deps=[t_fl, t_s16, t])
            t = cx.op("dve", lambda e: e.tensor_reduce(out=Lc[:, h, :], in_=flt[:], axis=mybir.AxisListType.X, op=ALU.add),
                      deps=[t])
            t = cx.op("dve", lambda e: e.tensor_scalar(out=Lc[:, h, :], in0=Lc[:, h, :], scalar1=bfs[:, h:h + 1], scalar2=None,
                                                       op0=ALU.add), deps=[t, t_bf])
        t = cx.op("act", lambda e: e.activation(out=Lc[:], in_=Lc[:], func=AF.Exp, scale=-1.0), deps=[t])
        t = cx.op("act", lambda e: e.activation(out=Lc[:], in_=Lc[:], func=AF.Ln, bias=one1[:, 0:1]), deps=[t, t_o1])
        Lf = Lc[:].rearrange("p h k -> p (h k)")
        cx.op("pe", lambda e: e.matmul(pC[:, 0, :], lhsT=triF[:], rhs=Lf, start=True, stop=True), deps=[t, t_tri])
        t_mm = cx.op("pe", lambda e: e.matmul(pC[:, 1, :], lhsT=onesF[:], rhs=Lf, start=True, stop=True))
        t = cx.op("dve", lambda e: e.tensor_copy(out=Tot[:].rearrange("p h k -> p (h k)"), in_=pC[:, 1, :]), deps=[t_mm])
        for h in range(NH):
            t = cx.op("dve", lambda e: e.tensor_tensor_scan(out=Exc[:, h, :], data0=onesF[:, 0:NKT], data1=Tot[:, h, :],
                                                            initial=0.0, op0=ALU.mult, op1=ALU.add), deps=[t])
        t = cx.op("dve", lambda e: e.tensor_tensor(out=Exc[:], in0=Exc[:], in1=Tot[:], op=ALU.subtract), deps=[t])
        t = cx.op("dve", lambda e: e.tensor_tensor(out=CL[:].rearrange("p h k -> p (h k)"), in0=pC[:, 0, :],
                                                   in1=Exc[:].rearrange("p h k -> p (h k)"), op=ALU.add), deps=[t])
        for h in range(NH):
            for ref in range(NKT):
                t = cx.op("dve", lambda e: e.tensor_scalar(out=Bias[:, h, ref, :], in0=CL[:, h, :], scalar1=Exc[:, h, ref:ref + 1],
                                                           scalar2=None, op0=ALU.subtract), deps=[t] if ref == 0 and h == 0 else [])
        t_bias = t
        for h in range(NH):
            for c in range(NKT // 4):
                t = cx.op("dve", lambda e: e.tensor_scalar(out=Fac[:, h, 4 * c:4 * c + 4], in0=Exc[:, h, 4 * c:4 * c + 4],
                                                           scalar1=Exc[:, h, 4 * c:4 * c + 1], scalar2=-1.0,
                                                           op0=ALU.subtract, op1=ALU.mult))
        t_fac = cx.op("act", lambda e: e.activation(out=Fac[:], in_=Fac[:], func=AF.Exp), deps=[t])

        head_done = [None, None]
        vT_free = None
        pS_free = [None] * NPS
        Pt_free = [None] * NPT
        fin_free = [None, None]
        pV_free = None
        acc_free = {"Oo": None, "Do": None, "Od": None, "Dd": None}
        outs = []
        gstep = 0
        for h in range(NH):
            hb = h % 2
            tk = [cx.idma(Kb[:, hb, a * 1024:(a + 1) * 1024], gK, idx3[:, h * 4 + a:h * 4 + a + 1],
                          key=f"ik{a}", deps=[in_deps, head_done[hb]]) for a in range(4)]
            tv = [cx.idma(vT[:, a * 1024:(a + 1) * 1024], gV, idx3[:, h * 4 + a:h * 4 + a + 1],
                          key=f"iv{a}", deps=[in_deps, vT_free]) for a in range(4)]
            tq = [cx.idma(Qb[:, hb, a * 1024:(a + 1) * 1024], gQ, idx3[:, h * 4 + a:h * 4 + a + 1],
                          key=f"iq{a}", deps=[in_deps, head_done[hb]]) for a in range(4)]
            t_vb = None
            last_tr = None
            for k4 in range(NKT // 4):
                for i in range(4):
                    kt = k4 * 4 + i
                    last_tr = cx.op("pe", lambda e: e.transpose(out=pV[:, i, :], in_=vT[:, kt * 128:(kt + 1) * 128], identity=identB[:]),
                                    deps=[tv, t_id, pV_free, head_done[hb]] if i == 0 else [])
                t_vb = cx.op("dve", lambda e: e.tensor_copy(out=Vb[:, hb, k4 * 4:k4 * 4 + 4, :], in_=pV[:, :, :]), deps=[last_tr])
                pV_free = t_vb
            vT_free = last_tr
            ld_deps = [tk, tq, t_vb]
            for qi in range(NQC):
                qc = qi
                steps = []
                for kt in range(4 * qc):
                    steps.append(dict(kt=kt, q0=qi * 512, n=512, c0=0, bias=Bias[:, h, 4 * qc, kt:kt + 1], mask=False,
                                      accO=pOo, accD=pDo, start=(kt == 0), stop=(kt == 4 * qc - 1), kind="o"))
                for rp in range(4):
                    for r in range(rp + 1):
                        kt = 4 * qc + r
                        steps.append(dict(kt=kt, q0=qi * 512 + rp * 128, n=128, c0=rp * 128,
                                          bias=Bias[:, h, 4 * qc + rp, kt:kt + 1], mask=(r == rp),
                                          accO=pOd, accD=pDd, start=(r == 0), stop=(r == rp), kind="d"))
                n = len(steps)
                exp_tok = [None] * n
                last_acc = {"o": None, "d": None}
                last_dacc = {"o": None, "d": None}
                first_acc_wait = {"o": [acc_free["Oo"], acc_free["Do"]], "d": [acc_free["Od"], acc_free["Dd"]]}
                for i in range(n + LOOK):
                    if i < n:
                        st = steps[i]
                        bi = (gstep + i) % NPS
                        sl = (gstep + i) % NPT
                        t_qk = cx.op("pe", lambda e: e.matmul(pS[bi][:, 0:st["n"]], lhsT=Kb[:, hb, st["kt"] * 128:(st["kt"] + 1) * 128],
                                                              rhs=Qb[:, hb, st["q0"]:st["q0"] + st["n"]], start=True, stop=True),
                                     deps=[pS_free[bi], ld_deps])
                        t_e = cx.op("act", lambda e: e.activation(out=Pt[:, sl, 0:st["n"]], in_=pS[bi][:, 0:st["n"]], func=AF.Exp,
                                                                  scale=ATT_SCALE, bias=st["bias"]),
                                    deps=[t_qk, Pt_free[sl], t_bias])
                        pS_free[bi] = t_e
                        if st["mask"]:
                            t_e = cx.op("dve", lambda e: e.tensor_tensor(out=Pt[:, sl, 0:128], in0=Pt[:, sl, 0:128], in1=triB[:],
                                                                         op=ALU.mult), deps=[t_e, t_triB])
                        exp_tok[i] = (t_e, sl)
                    if i >= LOOK:
                        st = steps[i - LOOK]
                        t_e, sl = exp_tok[i - LOOK]
                        c0, nn = st["c0"], st["n"]
                        fw = first_acc_wait[st["kind"]] if st["start"] and c0 == 0 else []
                        t_pv = cx.op("pe", lambda e: e.matmul(st["accO"][:, c0:c0 + nn], lhsT=Vb[:, hb, st["kt"], :], rhs=Pt[:, sl, 0:nn],
                                                              start=st["start"], stop=st["stop"]), deps=[t_e] + fw)
                        da = dacc[st["kind"]]
                        if st["start"]:
                            t_ac = cx.op("dve", lambda e: e.tensor_copy(out=da[:, c0:c0 + nn], in_=Pt[:, sl, 0:nn]),
                                         deps=[t_e, dacc_free[st["kind"]]])
                        else:
                            t_ac = cx.op("dve", lambda e: e.tensor_tensor(out=da[:, c0:c0 + nn], in0=da[:, c0:c0 + nn],
                                                                          in1=Pt[:, sl, 0:nn], op=ALU.add), deps=[t_e])
                        Pt_free[sl] = [t_pv, t_ac]
                        last_acc[st["kind"]] = t_pv
                        last_dacc[st["kind"]] = t_ac
                gstep += n
                fs = (h * NQC + qi) % 2
                t_dm = cx.op("pe", lambda e: e.matmul(pDen[:, :], lhsT=onesF[:], rhs=dacc["d"][:, :], start=True, stop=True),
                             deps=[last_dacc["d"], tp, acc_free["Dd"]])
                dacc_free["d"] = t_dm
                cx.op("act", lambda e: e.activation(out=Od[:, 0, :], in_=pOd[:, :], func=AF.Copy),
                      deps=[last_acc["d"]] + outs[-2:])
                t_c2 = cx.op("act", lambda e: e.activation(out=Od[:, 1, :], in_=pDen[:, :], func=AF.Copy), deps=[t_dm])
                acc_free["Od"] = t_c2
                if qc > 0:
                    t_dm2 = cx.op("pe", lambda e: e.matmul(pDen[:, :], lhsT=onesF[:], rhs=dacc["o"][:, :], start=True, stop=True),
                                  deps=[last_dacc["o"], t_c2])
                    dacc_free["o"] = t_dm2
                    t_a = None
                    for rp in range(4):
                        cs = slice(rp * 128, (rp + 1) * 128)
                        fc = Fac[:, h, 4 * qc + rp:4 * qc + rp + 1]
                        cx.op("dve", lambda e: e.scalar_tensor_tensor(out=ofin[:, fs, cs], in0=pOo[:, cs], scalar=fc, in1=Od[:, 0, cs],
                                                                      op0=ALU.mult, op1=ALU.add), deps=[t_c2, last_acc["o"], t_fac])
                        t_a = cx.op("dve", lambda e: e.scalar_tensor_tensor(out=dfin[:, fs, cs], in0=pDen[:, cs], scalar=fc,
                                                                            in1=Od[:, 1, cs], op0=ALU.mult, op1=ALU.add), deps=[t_dm2])
                    acc_free["Oo"] = t_a
                    acc_free["Dd"] = t_a
                    t_r = cx.op("dve", lambda e: e.reciprocal(out=dfin[:, fs, :], in_=dfin[:, fs, :]), deps=[t_a])
                    t_f = cx.op("dve", lambda e: e.tensor_tensor(out=obf[:, fs, :], in0=ofin[:, fs, :], in1=dfin[:, fs, :],
                                                                 op=ALU.mult), deps=[t_r, fin_free[fs]])
                else:
                    acc_free["Dd"] = t_c2
                    t_r = cx.op("dve", lambda e: e.reciprocal(out=dfin[:, fs, :], in_=Od[:, 1, :]), deps=[t_c2])
                    t_f = cx.op("dve", lambda e: e.tensor_tensor(out=obf[:, fs, :], in0=Od[:, 0, :], in1=dfin[:, fs, :],
                                                                 op=ALU.mult), deps=[t_r, fin_free[fs]])
                a, half = qc // 2, qc % 2
                t_o = cx.dma("sp", bo[a * 512 + h * 128:a * 512 + (h + 1) * 128, half * 512:(half + 1) * 512], obf[:, fs, :],
                             key=f"o{fs}", deps=[t_f])
                fin_free[fs] = t_o
                outs.append(t_f)
            head_done[hb] = cx.op("pe", lambda e: e.matmul(pS[0][0:1, 0:1], lhsT=onesB[0:1, 0:1], rhs=onesB[0:1, 0:1],
                                                           start=True, stop=True), deps=[pS_free[0]])
            pS_free[0] = head_done[hb]
        cx.barrier()


def build_fused():
    nc = bass.Bass("TRN2", target_bir_lowering=False)
    I32 = mybir.dt.int32
    ext = lambda n, sh, dt=F32: nc.dram_tensor(n, sh, dt, kind="ExternalInput").ap()
    xT = ext("xT", [D, TOK])
    gcols = ext("gcols", [128, 6, 16])
    w_in, w_glu, w_out0 = ext("w_in", [D, 2 * D]), ext("w_glu", [D, D]), ext("w_out0", [D, D])
    kv_w, fw_in, fw_out = ext("kv_w", [D, 2 * D + 16]), ext("fw_in", [D, 2 * D]), ext("fw_out", [D, D])
    pars, Bri, Cri = ext("pars", [64, 3, 16]), ext("Bri", [64, 2, 16, 16]), ext("Cri", [64, 2, 16, 16])
    dcol, BFd, sel16 = ext("dcol", [128, 16]), ext("BF", [128, 4]), ext("sel16", [128, 4, 16])
    idxd = ext("idx", [128, 68], I32)
    outT = nc.dram_tensor("outT", [D, TOK], F32, kind="ExternalOutput").ap()
    bu_t, gu_t = nc.dram_tensor("bu", [D, TOK], BF16), nc.dram_tensor("gu", [8 * D, TOK], BF16)
    szd = nc.dram_tensor("szd", [D, TOK], BF16).ap()
    by_t, gy_t = nc.dram_tensor("by", [D, TOK], BF16), nc.dram_tensor("gy", [8 * D, TOK], BF16)
    h1d = nc.dram_tensor("h1d", [D, TOK], F32).ap()
    bK_t, gK_t = nc.dram_tensor("bK", [D, TOK], BF16), nc.dram_tensor("gK", [8 * D, TOK], BF16)
    bV_t, gV_t = nc.dram_tensor("bV", [D, TOK], BF16), nc.dram_tensor("gV", [8 * D, TOK], BF16)
    bQ_t, gQ_t = nc.dram_tensor("bQ", [D, TOK], BF16), nc.dram_tensor("gQ", [8 * D, TOK], BF16)
    bfl_t, gfl_t = nc.dram_tensor("bfl", [128, 128], F32), nc.dram_tensor("gfl", [8 * 128, 128], F32)
    sz2d = nc.dram_tensor("sz2d", [D, TOK], BF16).ap()
    bo_t, go_t = nc.dram_tensor("bo", [D, TOK], BF16), nc.dram_tensor("go", [8 * D, TOK], BF16)
    bu, gu, by, gy, bK, gK, bV, gV, bQ, gQ, bfl, gfl, bo, go = [t.ap() for t in (
        bu_t, gu_t, by_t, gy_t, bK_t, gK_t, bV_t, gV_t, bQ_t, gQ_t, bfl_t, gfl_t, bo_t, go_t)]
    with ExitStack() as es:
        cx = Ctx(nc, es)
        gcs = cx.sb("gcs", [128, 6, 16], F32)
        idx = cx.sb("idxs", [128, 68], I32)
        t_c = [cx.dma("sp", gcs[:], gcols[:, :, :], key="g"), cx.dma("sp", idx[:], idxd[:, :], key="g")]
        r1 = lambda ft: (bu, ft * 128, AF.Copy) if ft < 16 else (szd, (ft - 16) * 128, AF.Silu)
        ccs = {}

        def mk_hook(sched):
            def hook(ft):
                if ft in sched:
                    nm, a, b = sched[ft]
                    ccs[nm] = cx.cc("AllGather", GROUPS8, a.ap().opt(), b.ap().opt(), deps=cx.dma_tokens())
            return hook
        stage_norm_linear(cx, xT, [(gcs[:, 0, :], t_c, w_in, 2 * D, r1, mk_hook({18: ("u", bu_t, gu_t)}))])
        cc1 = ccs["u"]
        stage_s5(cx, gu, idx[:, 0:16], [t_c, cc1], pars, Bri, Cri, dcol, by.rearrange("(f r) t -> f (r t)", r=8))
        cc2 = cx.cc("AllGather", GROUPS8, by_t.ap().opt(), gy_t.ap().opt())
        stage_gate_out(cx, True, gy, idx[:, 16:32], szd, xT, gcs[:, 1, :], gcs[:, 5, :], [t_c, cc2], w_out0, w_glu, h1d, True)
        rk = lambda ft: ((bK, ft * 128, AF.Copy) if ft < 16 else (bV, (ft - 16) * 128, AF.Copy)) if ft < 32 else None
        rq = lambda ft: (bQ, ft * 128, AF.Copy) if ft < 16 else (sz2d, (ft - 16) * 128, AF.Silu)
        stage_norm_linear(cx, h1d, [(gcs[:, 2, :], t_c, kv_w, 2 * D + 16, rk, mk_hook({18: ("k", bK_t, gK_t)})),
                                    (gcs[:, 3, :], t_c, fw_in, 2 * D, rq,
                                     mk_hook({2: ("v", bV_t, gV_t), 3: ("f", bfl_t, gfl_t), 18: ("q", bQ_t, gQ_t)}))], bfl=bfl)
        cc3 = [ccs["k"], ccs["v"], ccs["f"], ccs["q"]]
        stage_attn(cx, gK, gV, gQ, idx[:, 32:48], gfl, idx[:, 48:52], sel16, BFd, [t_c, cc3], bo)
        cc4 = cx.cc("AllGather", GROUPS8, bo_t.ap().opt(), go_t.ap().opt())
        stage_gate_out(cx, False, go, idx[:, 52:68], sz2d, h1d, gcs[:, 4, :], None, [t_c, cc4], fw_out, None, outT, False)
    return nc


def kernel(x, norm_pre, norm_post, s5_w_in, s5_a_re, s5_a_im, s5_log_dt, s5_b_re, s5_b_im,
           s5_c_re, s5_c_im, s5_d, s5_w_glu, s5_b_glu, s5_w_out, kv_norm, kv_w, kv_b_f,
           fox_w_in, fox_w_out):
    f = lambda a: np.ascontiguousarray(np.asarray(a), dtype=np.float32)
    x = f(x)
    B, S, _ = x.shape
    gc = np.stack([f(norm_pre)[0], f(norm_post)[0], f(kv_norm), f(norm_pre)[1], f(norm_post)[1], f(s5_b_glu)[0]])
    gcols = f(gc.reshape(6, 16, 128).transpose(2, 0, 1))
    shared = {"gcols": gcols, "w_in": f(s5_w_in)[0], "w_glu": f(s5_w_glu)[0], "w_out0": f(s5_w_out)[0],
              "kv_w": f(kv_w), "fw_in": f(fox_w_in)[0], "fw_out": f(fox_w_out)[0]}
    a_re, a_im, ldt = f(s5_a_re)[0], f(s5_a_im)[0], f(s5_log_dt)[0]
    b_re, b_im, c_re, c_im, dd = f(s5_b_re)[0], f(s5_b_im)[0], f(s5_c_re)[0], f(s5_c_im)[0], f(s5_d)[0]
    bfv = f(kv_b_f)
    p = np.arange(128, dtype=np.int64)[:, None]
    ins = []
    for c in range(NCORES):
        b, sl = c // 4, c % 4
        hg = sl
        xs = x[b, sl * TOK:(sl + 1) * TOK]
        xT = f(xs.reshape(128, 8, D).transpose(1, 0, 2).reshape(TOK, D).T)
        gs = slice(c * 16, c * 16 + 16)
        pars = np.stack([a_re[gs].T, a_im[gs].T, np.broadcast_to(ldt[gs][None, :], (64, 16))], axis=1)
        Bri = np.stack([b_re[gs].transpose(1, 0, 2), b_im[gs].transpose(1, 0, 2)], axis=1)
        Cri = np.stack([c_re[gs].transpose(2, 0, 1), c_im[gs].transpose(2, 0, 1)], axis=1)
        dcol = np.tile(dd.reshape(128, 16)[gs].T, (8, 1))
        sel16 = np.zeros((128, 4, 16), np.float32)
        for h in range(4):
            sel16[:, h, hg * 4 + h] = 1.0
        idx = np.zeros((128, 68), np.int64)
        for r in range(8):
            for ftl in range(2):
                idx[:, r * 2 + ftl] = (r * D + c * 256 + ftl * 128 + p)[:, 0]
        for ft in range(16):
            idx[:, 16 + ft] = ((ft * 128 + p) * 8 + c)[:, 0]
        for h in range(4):
            for a in range(4):
                idx[:, 32 + h * 4 + a] = ((b * 4 + a) * D + hg * 512 + h * 128 + p)[:, 0]
        for a in range(4):
            idx[:, 48 + a] = ((b * 4 + a) * 128 + p)[:, 0]
        for hg2 in range(4):
            for ftl in range(4):
                idx[:, 52 + hg2 * 4 + ftl] = ((b * 4 + hg2) * D + sl * 512 + ftl * 128 + p)[:, 0]
        ins.append(dict(shared, xT=xT, pars=f(pars), Bri=f(Bri), Cri=f(Cri), dcol=f(dcol),
                        BF=f(np.broadcast_to(bfv[hg * 4:hg * 4 + 4][None, :], (128, 4))), sel16=sel16,
                        idx=np.ascontiguousarray(idx.astype(np.int32))))
    nc = build_fused()
    res = run_bass_kernel_spmd(nc, ins, core_ids=list(range(NCORES)))
    out = np.zeros((B, S, D), np.float32)
    for c in range(NCORES):
        b, sl = c // 4, c % 4
        out[b, sl * TOK:(sl + 1) * TOK] = res.results[c]["outT"].T
    return out
```
